# Optimizing a Trainium2 kernel written in Bass

```python
import jax, jax.numpy as jnp
from jax import lax
import numpy as np

D_MODEL = 2048
BATCH = 8
SEQ = 4096
DEPTH = 2

GRID_W = 64
HEAD_DIM = 128
N_HEADS = 16
N_KV = 4
GROUP = N_HEADS // N_KV
Q_WIDTH = N_HEADS * HEAD_DIM
KV_WIDTH = N_KV * HEAD_DIM
AXIS_DIM = HEAD_DIM // 2
ROPE_THETA = 10000.0
Q_BLOCK = 128
N_FOURIER_GROUPS = 4
FOURIER_GROUP_DIM = 256
F_WIDTH = N_FOURIER_GROUPS * FOURIER_GROUP_DIM
IN_WIDTH = Q_WIDTH + 2 * KV_WIDTH + F_WIDTH
N_BRANCHES = 2
MEM_TOKENS = 256
X_HEADS = 4
X_HEAD_DIM = 128
X_WIDTH = X_HEADS * X_HEAD_DIM
D_FF = 5632
CONV_WIDTH = 3
EPS = 1e-6

kernel_name = "hybrid_gqa_fourier_memxattn_convffn_encoder"


def rms_norm(x, g):
    xf = x.astype(jnp.float32)
    y = xf * lax.rsqrt(jnp.mean(xf * xf, axis=-1, keepdims=True) + EPS)
    return (y * g.astype(jnp.float32)).astype(x.dtype)


def axial_rope_angles(seq_len):
    rows = seq_len // GRID_W
    row_ids = jnp.repeat(jnp.arange(rows), GRID_W).astype(jnp.float32)
    col_ids = jnp.tile(jnp.arange(GRID_W), rows).astype(jnp.float32)
    inv_freq = 1.0 / (ROPE_THETA ** (jnp.arange(0, AXIS_DIM, 2, dtype=jnp.float32) / AXIS_DIM))
    ang = jnp.stack([row_ids[:, None] * inv_freq, col_ids[:, None] * inv_freq], axis=1)
    return jnp.cos(ang), jnp.sin(ang)


def apply_axial_rope(x, cos, sin):
    b, s, h, _ = x.shape
    xf = x.astype(jnp.float32).reshape(b, s, h, 2, AXIS_DIM)
    x1, x2 = xf[..., : AXIS_DIM // 2], xf[..., AXIS_DIM // 2:]
    c = cos[None, :, None, :, :]
    sn = sin[None, :, None, :, :]
    out = jnp.concatenate([x1 * c - x2 * sn, x2 * c + x1 * sn], axis=-1)
    return out.reshape(b, s, h, HEAD_DIM).astype(x.dtype)


def blocked_gqa(q, k, v):
    b, s = q.shape[0], q.shape[1]
    nb = s // Q_BLOCK
    qb = q.reshape(b, nb, Q_BLOCK, N_KV, GROUP, HEAD_DIM).transpose(1, 0, 2, 3, 4, 5)
    scale = HEAD_DIM ** -0.5

    def one_block(qi):
        sc = jnp.einsum('bqhgd,bkhd->bhgqk', qi, k, preferred_element_type=jnp.float32) * scale
        p = jax.nn.softmax(sc, axis=-1).astype(v.dtype)
        return jnp.einsum('bhgqk,bkhd->bqhgd', p, v)

    ob = lax.map(one_block, qb)
    return ob.transpose(1, 0, 2, 3, 4, 5).reshape(b, s, Q_WIDTH)


def fourier_mix(u):
    b, s = u.shape[0], u.shape[1]
    ug = u.astype(jnp.float32).reshape(b, s, N_FOURIER_GROUPS, FOURIER_GROUP_DIM)
    y = jnp.fft.fft2(ug, axes=(1, 3), norm='ortho').real
    return y.reshape(b, s, F_WIDTH).astype(u.dtype)


def token_mixer(h, cos, sin, w_in, q_norm_g, k_norm_g, w_attn_o, w_four_o, w_gate, b_gate, w_mix_o):
    b, s = h.shape[0], h.shape[1]
    proj = h @ w_in
    q = proj[..., :Q_WIDTH].reshape(b, s, N_HEADS, HEAD_DIM)
    k = proj[..., Q_WIDTH:Q_WIDTH + KV_WIDTH].reshape(b, s, N_KV, HEAD_DIM)
    v = proj[..., Q_WIDTH + KV_WIDTH:Q_WIDTH + 2 * KV_WIDTH].reshape(b, s, N_KV, HEAD_DIM)
    uf = proj[..., Q_WIDTH + 2 * KV_WIDTH:]
    q = apply_axial_rope(rms_norm(q, q_norm_g), cos, sin)
    k = apply_axial_rope(rms_norm(k, k_norm_g), cos, sin)
    a = blocked_gqa(q, k, v) @ w_attn_o
    f = fourier_mix(uf) @ w_four_o
    g = jax.nn.sigmoid(h @ w_gate + b_gate)
    merged = g[..., :D_MODEL] * a + g[..., D_MODEL:] * f
    return merged @ w_mix_o


def memory_xattn(h, mem_n, w_xq, w_xkv, w_xo):
    b, s = h.shape[0], h.shape[1]
    m = mem_n.shape[1]
    q = (h @ w_xq).reshape(b, s, X_HEADS, X_HEAD_DIM)
    kv = (mem_n @ w_xkv).reshape(b, m, 2, X_HEADS, X_HEAD_DIM)
    k, v = kv[:, :, 0], kv[:, :, 1]
    sc = jnp.einsum('bqhd,bkhd->bhqk', q, k, preferred_element_type=jnp.float32) * (X_HEAD_DIM ** -0.5)
    p = jax.nn.softmax(sc, axis=-1).astype(v.dtype)
    o = jnp.einsum('bhqk,bkhd->bqhd', p, v).reshape(b, s, X_WIDTH)
    return o @ w_xo


def conv_ffn(h, w_up, conv_w, conv_b, w_down):
    u = h @ w_up
    up = jnp.pad(u, ((0, 0), (1, 1), (0, 0)))
    u = up[:, :-2] * conv_w[0] + up[:, 1:-1] * conv_w[1] + up[:, 2:] * conv_w[2] + conv_b
    gate, val = u[..., :D_FF], u[..., D_FF:]
    return (jax.nn.gelu(gate, approximate=True) * val) @ w_down


def setup_inputs(seed: int = 0) -> dict:
    key = jax.random.key(seed)
    ks = jax.random.split(key, 24)
    L = DEPTH

    def dense(k, shape, fan_in):
        return jax.random.normal(k, shape, jnp.float32) * (fan_in ** -0.5)

    def gain(k, shape):
        return 1.0 + 0.02 * jax.random.normal(k, shape, jnp.float32)

    return {
        'x': jax.random.normal(ks[0], (BATCH, SEQ, D_MODEL), jnp.float32),
        'mem': jax.random.normal(ks[1], (BATCH, MEM_TOKENS, D_MODEL), jnp.float32),
        'mix_pre_g': gain(ks[2], (L, D_MODEL)),
        'w_in': dense(ks[3], (L, D_MODEL, IN_WIDTH), D_MODEL),
        'q_norm_g': gain(ks[4], (L, HEAD_DIM)),
        'k_norm_g': gain(ks[5], (L, HEAD_DIM)),
        'w_attn_o': dense(ks[6], (L, Q_WIDTH, D_MODEL), Q_WIDTH),
        'w_four_o': dense(ks[7], (L, F_WIDTH, D_MODEL), F_WIDTH),
        'w_gate': dense(ks[8], (L, D_MODEL, N_BRANCHES * D_MODEL), D_MODEL),
        'b_gate': 0.01 * jax.random.normal(ks[9], (L, N_BRANCHES * D_MODEL), jnp.float32),
        'w_mix_o': dense(ks[10], (L, D_MODEL, D_MODEL), D_MODEL),
        'mix_post_g': gain(ks[11], (L, D_MODEL)),
        'xa_pre_g': gain(ks[12], (L, D_MODEL)),
        'mem_norm_g': gain(ks[13], (L, D_MODEL)),
        'w_xq': dense(ks[14], (L, D_MODEL, X_WIDTH), D_MODEL),
        'w_xkv': dense(ks[15], (L, D_MODEL, 2 * X_WIDTH), D_MODEL),
        'w_xo': dense(ks[16], (L, X_WIDTH, D_MODEL), X_WIDTH),
        'xa_post_g': gain(ks[17], (L, D_MODEL)),
        'ffn_pre_g': gain(ks[18], (L, D_MODEL)),
        'w_up': dense(ks[19], (L, D_MODEL, 2 * D_FF), D_MODEL),
        'conv_w': dense(ks[20], (L, CONV_WIDTH, 2 * D_FF), CONV_WIDTH),
        'conv_b': 0.01 * jax.random.normal(ks[21], (L, 2 * D_FF), jnp.float32),
        'w_down': dense(ks[22], (L, D_FF, D_MODEL), D_FF),
        'ffn_post_g': gain(ks[23], (L, D_MODEL)),
    }


def reference(x, mem, mix_pre_g, w_in, q_norm_g, k_norm_g, w_attn_o, w_four_o, w_gate, b_gate,
              w_mix_o, mix_post_g, xa_pre_g, mem_norm_g, w_xq, w_xkv, w_xo, xa_post_g,
              ffn_pre_g, w_up, conv_w, conv_b, w_down, ffn_post_g):
    seq_len = x.shape[1]
    cos, sin = axial_rope_angles(seq_len)
    for l in range(DEPTH):
        h = rms_norm(x, mix_pre_g[l])
        y = token_mixer(h, cos, sin, w_in[l], q_norm_g[l], k_norm_g[l], w_attn_o[l],
                        w_four_o[l], w_gate[l], b_gate[l], w_mix_o[l])
        x = x + rms_norm(y, mix_post_g[l])

        h = rms_norm(x, xa_pre_g[l])
        mem_n = rms_norm(mem, mem_norm_g[l])
        y = memory_xattn(h, mem_n, w_xq[l], w_xkv[l], w_xo[l])
        x = x + rms_norm(y, xa_post_g[l])

        h = rms_norm(x, ffn_pre_g[l])
        y = conv_ffn(h, w_up[l], conv_w[l], conv_b[l], w_down[l])
        x = x + rms_norm(y, ffn_post_g[l])
    return x
```

```python
import numpy as np
import ml_dtypes
from contextlib import ExitStack
import concourse.bass as bass
import concourse.mybir as mybir
from concourse.bass_utils import run_bass_kernel_spmd

F32 = mybir.dt.float32
BF16 = mybir.dt.bfloat16
AF = mybir.ActivationFunctionType
ALU = mybir.AluOpType

S = 4096
D = 2048
NCORES = 8
SAME_ENGINE_SYNC = True


def OP(name, *a, **k):
    return (name, a, k)


class T:
    __slots__ = ("w", "r")

    def __init__(self):
        self.w = None
        self.r = {}


class Sem:
    __slots__ = ("h", "cnt", "name")

    def __init__(self, h, name):
        self.h = h
        self.cnt = 0
        self.name = name


class Sched:
    ENG = ("pe", "act", "dve", "pool", "sp")

    def __init__(self, nc, stack):
        self.nc = nc
        self.stack = stack
        self.ops = {e: [] for e in self.ENG}
        self.esem = {e: Sem(stack.enter_context(nc.semaphore(f"es_{e}")), e) for e in self.ENG}
        self.seen = {e: {} for e in self.ENG}
        self.dsems = []
        self.nops = 0
        self.limit = 1 << 60

    def dma_sem(self, name):
        s = Sem(self.stack.enter_context(self.nc.semaphore(f"ds_{name}")), name)
        self.dsems.append(s)
        return s

    def _waits(self, eng, reads, writes):
        evs = {}
        for t in reads:
            if t.w is not None:
                s, v = t.w
                if evs.get(s, 0) < v:
                    evs[s] = v
        for t in writes:
            if t.w is not None:
                s, v = t.w
                if evs.get(s, 0) < v:
                    evs[s] = v
            for s, v in t.r.items():
                if evs.get(s, 0) < v:
                    evs[s] = v
        own = self.esem[eng]
        out = []
        seen = self.seen[eng]
        for s, v in evs.items():
            if s is own and (eng == "pe" or not SAME_ENGINE_SYNC):
                continue
            if seen.get(s, 0) >= v:
                continue
            seen[s] = v
            out.append((s.h, v))
        return out

    @staticmethod
    def _commit(ev, reads, writes):
        s, v = ev
        for t in reads:
            if t.r.get(s, 0) < v:
                t.r[s] = v
        for t in writes:
            t.w = ev
            t.r = {}

    def op(self, eng, fn, reads=(), writes=(), signal=True):
        assert signal or eng == "pe"
        if self.nops >= self.limit:
            return
        waits = self._waits(eng, reads, writes)
        es = self.esem[eng]
        ev = (es, es.cnt + 1)
        if signal:
            es.cnt += 1
        self.ops[eng].append((waits, fn, (es.h, 1) if signal else None))
        self._commit(ev, reads, writes)
        self.nops += 1

    def dma(self, q, dsem, out_ap, in_ap, reads=(), writes=()):
        if self.nops >= self.limit:
            return
        waits = self._waits(q, reads, writes)
        dsem.cnt += 16
        ev = (dsem, dsem.cnt)
        self.ops[q].append((waits, OP("dma_start", out=out_ap, in_=in_ap), (dsem.h, 16)))
        self._commit(ev, reads, writes)
        self.nops += 1

    def finish(self):
        waits = []
        for s in self.dsems:
            if s.cnt:
                waits.append((s.h, s.cnt))
        for e in self.ENG:
            if e != "sp" and self.esem[e].cnt:
                waits.append((self.esem[e].h, self.esem[e].cnt))
        self.ops["sp"].append((waits, None, None))

    def emit(self):
        nc = self.nc
        ops = self.ops

        def replay(name, eng):
            for waits, fn, sig in ops[name]:
                for h, v in waits:
                    eng.wait_ge(h, v)
                if fn is None:
                    continue
                inst = getattr(eng, fn[0])(*fn[1], **fn[2])
                if sig is not None:
                    inst.then_inc(sig[0], sig[1])

        self.finish()
        with nc.Block() as block:
            @block.sync
            def _(e):
                replay("sp", e)

            @block.scalar
            def _(e):
                replay("act", e)

            @block.vector
            def _(e):
                replay("dve", e)

            @block.gpsimd
            def _(e):
                replay("pool", e)

            @block.tensor
            def _(e):
                replay("pe", e)
        self.ops = {e: [] for e in self.ENG}


L = 2
DFF = 5632
NFF = DFF // 128
EPS = 1e-6
VEC_SPEC = [
    ("mix_pre_g", 16), ("mix_post_g", 16), ("xa_pre_g", 16), ("xa_post_g", 16),
    ("ffn_pre_g", 16), ("ffn_post_g", 16), ("mem_norm_g", 16), ("b_gate", 32),
    ("q_norm_g", 1), ("k_norm_g", 1), ("conv_w0", 88), ("conv_w1", 88), ("conv_w2", 88), ("conv_b", 88),
]


def vec_layout():
    lay = {}
    c = 0
    for l in range(L):
        for nm, n in VEC_SPEC:
            lay[(nm, l)] = c
            c += n
    return lay, c


def host_tables():
    pos = np.arange(S)
    row = (pos // 64).astype(np.float64)
    col = (pos % 64).astype(np.float64)
    inv = 1.0 / (10000.0 ** (np.arange(0, 64, 2, dtype=np.float32) / 64)).astype(np.float64)
    p = np.arange(128)
    axis = p // 64
    fi = p % 32
    posax = np.where(axis[:, None] == 0, row[None, :], col[None, :])
    ang = (posax.astype(np.float32) * inv[fi][:, None].astype(np.float32)).astype(np.float32)
    ropeC = np.cos(ang).astype(np.float32)
    sgn = np.where((p % 64) < 32, -1.0, 1.0)[:, None]
    ropeS = (np.sin(ang) * sgn).astype(np.float32)
    partner = np.where((p % 64) < 32, p + 32, p - 32)
    perm = np.zeros((128, 128), np.float32)
    perm[partner, p] = 1.0
    ones = np.ones((128, 128), np.float32)
    n = np.arange(S, dtype=np.int64)
    ph = (np.outer(n, n) % S).astype(np.float64) * (2 * np.pi / S)
    dftCS = np.concatenate([np.cos(ph) / 64.0, -np.sin(ph) / 64.0], axis=0).astype(ml_dtypes.bfloat16)
    c = np.arange(256, dtype=np.int64)
    ph2 = (np.outer(c, c) % 256).astype(np.float64) * (2 * np.pi / 256)
    cs256 = np.concatenate([np.cos(ph2) / 16.0, np.sin(ph2) / 16.0], axis=1).astype(ml_dtypes.bfloat16)
    return dict(ropeC=ropeC, ropeS=ropeS, perm=perm.astype(ml_dtypes.bfloat16),
                ones=ones.astype(ml_dtypes.bfloat16), dftCS=dftCS, cs256=cs256)


def pack_vecs(inp):
    lay, nv = vec_layout()
    out = np.zeros((128, nv), np.float32)
    for l in range(L):
        for nm, n in VEC_SPEC:
            if nm.startswith("conv_w"):
                v = inp["conv_w"][l, int(nm[-1])]
            elif nm in ("q_norm_g", "k_norm_g"):
                v = inp[nm][l]
            else:
                v = inp[nm][l]
            c0 = lay[(nm, l)]
            out[:, c0:c0 + n] = np.asarray(v, np.float32).reshape(n, 128).T
    return out


class Ctx:
    pass


def build_program(n_layers=L, debug_outs=(), stop_after=None, only=None, limit=None):
    nc = bass.Bass("TRN2", target_bir_lowering=False)
    ctx = Ctx()
    ctx.nc = nc
    lay, NV = vec_layout()
    dr = {}

    def din(name, shape, dt):
        dr[name] = nc.dram_tensor(name, list(shape), dt, kind="ExternalInput").ap()

    def dscr(name, shape, dt):
        kind = "ExternalOutput" if name in debug_outs else "Internal"
        dr[name] = nc.dram_tensor(name, list(shape), dt, kind=kind).ap()

    din("xT", [D, S], F32)
    din("memT", [D, 256], F32)
    din("vecs", [128, NV], F32)
    din("ropeC", [128, S], F32)
    din("ropeS", [128, S], F32)
    din("perm", [128, 128], BF16)
    din("ones", [128, 128], BF16)
    din("dftCS", [2 * S, S], BF16)
    din("cs256", [256, 512], BF16)
    for nm, shp in [("w_in", [L, D, 4096]), ("w_attn_o", [L, D, D]), ("w_four_o", [L, 1024, D]),
                    ("w_gate", [L, D, 4096]), ("w_mix_o", [L, D, D]), ("w_xq", [L, D, 512]),
                    ("w_xkv", [L, D, 1024]), ("w_xo", [L, 512, D]), ("w_up", [L, D, 2 * DFF]),
                    ("w_down", [L, DFF, D])]:
        din(nm, shp, F32)
    dr["outT"] = nc.dram_tensor("outT", [D, S], F32, kind="ExternalOutput").ap()
    for nm, shp, dt in [("xa", [D, S], F32), ("xb", [D, S], F32), ("hT", [D, S], BF16), ("qT", [D, S], BF16),
                        ("kT", [512, S], BF16), ("vtm", [S, 512], BF16), ("ufT", [1024, S], BF16),
                        ("gT", [4096, S], BF16), ("oT", [D, S], BF16), ("YT", [1024, S], BF16),
                        ("m1T", [D, S], BF16), ("mT", [D, S], BF16), ("yT", [D, S], F32),
                        ("memnT", [D, 256], BF16), ("qxT", [512, S], BF16), ("kxT", [512, 256], BF16),
                        ("vxtm", [256, 512], BF16), ("oxT", [512, S], BF16), ("actT", [DFF, S], BF16), ("abS", [2 * S, 1024], BF16)]:
        dscr(nm, shp, dt)

    dtiles = {}

    def dts(name, rows, tbs):
        out = []
        for r in rows:
            for tb in tbs:
                key = (name, r, tb)
                t = dtiles.get(key)
                if t is None:
                    t = dtiles[key] = T()
                out.append(t)
        return out

    uid = [0]
    with ExitStack() as top:
        sc = Sched(nc, top)
        ctx.sc = sc
        if limit is not None:
            sc.limit = limit
        dpool = [sc.dma_sem(f"p{i}") for i in range(56)]
        dnext = [0]

        def dsem():
            s_ = dpool[dnext[0] % len(dpool)]
            dnext[0] += 1
            return s_

        def alloc(st, shape, dt, nm="t"):
            uid[0] += 1
            return st.enter_context(nc.sbuf_tensor(f"{nm}_{uid[0]}", list(shape), dt))

        vecs = alloc(top, [128, NV], F32, "vecs")
        ones = alloc(top, [128, 128], BF16, "ones")
        perm = alloc(top, [128, 128], BF16, "perm")
        Tconst = T()
        sc.dma("sp", dsem(), vecs[:, :], dr["vecs"], writes=[Tconst])
        sc.dma("sp", dsem(), ones[:, :], dr["ones"], writes=[Tconst])
        sc.dma("sp", dsem(), perm[:, :], dr["perm"], writes=[Tconst])
        psum = [top.enter_context(nc.psum_tensor(f"ps{i}", [128, 512], F32)) for i in range(8)]
        Tpsum = [T() for _ in range(8)]
        pcnt = [0]

        def next_psum(lo=0, hi=8):
            i = lo + pcnt[0] % (hi - lo)
            pcnt[0] += 1
            return psum[i], Tpsum[i]

        def vcol(nm, l, k=0):
            c = lay[(nm, l)] + k
            return vecs[:, c:c + 1]

        class Ring:
            def __init__(self, st, n, shape, dt, nm="r", dma=False):
                self.b = [alloc(st, shape, dt, nm) for _ in range(n)]
                self.t = [T() for _ in range(n)]
                self.d = [dsem() for _ in range(n)] if dma else None
                self.i = 0

            def next(self):
                i = self.i % len(self.b)
                self.i += 1
                return (self.b[i], self.t[i], self.d[i]) if self.d else (self.b[i], self.t[i])

        alt = [0]

        def evac_copy(out_ap, ps, Tps, Tout):
            alt[0] += 1
            if alt[0] % 2:
                sc.op("act", OP("activation", out=out_ap, in_=ps, func=AF.Copy), reads=[Tconst], writes=[Tps, Tout])
            else:
                sc.op("dve", OP("tensor_copy", out=out_ap, in_=ps), writes=[Tps, Tout])

        def phase_end():
            dnext[0] = 0
            sc.emit()

        def rstd_from_sq(st_ring_r, sq_list, Tsq, n_red, scale, recip=True):
            ps, Tps = next_psum()
            n = len(sq_list)
            for i, ap in enumerate(sq_list):
                sc.op("pe", OP("matmul", ps[:, :], lhsT=ones[:, :], rhs=ap, start=(i == 0), stop=(i == n - 1)),
                      reads=[Tconst, Tsq], writes=[Tps], signal=(i == n - 1))
            r, Tr = st_ring_r.next()
            sc.op("act", OP("activation", out=r[:, :], in_=ps[:, :], func=AF.Sqrt, scale=scale, bias=EPS), writes=[Tps, Tr])
            if recip:
                sc.op("dve", OP("reciprocal", out=r[:, :], in_=r[:, :]), writes=[Tr])
            return r, Tr

        def post_phase(l, y_name, g_post, x_in, x_out, g_pre, h_out, l_pre=None, S_tot=S):
            l_pre = l if l_pre is None else l_pre
            with ExitStack() as st:
                xr = Ring(st, 2, [128, 16, 512], F32, "x", dma=True)
                yr = Ring(st, 2, [128, 16, 512], F32, "y", dma=True) if y_name else None
                sqr = Ring(st, 1, [128, 16, 512], BF16, "sq")
                hr = Ring(st, 1, [128, 16, 512], BF16, "h", dma=True)
                rr = Ring(st, 3, [128, 512], F32, "rs")
                live = {}

                def stage_a(tb):
                    tsl = slice(tb * 512, (tb + 1) * 512)
                    x, Tx, Dx = xr.next()
                    sc.dma("sp", Dx, x[:, :, :], dr[x_in][:, tsl].rearrange("(k p) t -> p k t", p=128),
                           reads=dts(x_in, range(16), [tb]), writes=[Tx])
                    live[tb] = (x, Tx)
                    if not y_name:
                        return
                    y, Ty, Dy = yr.next()
                    sc.dma("sp", Dy, y[:, :, :], dr[y_name][:, tsl].rearrange("(k p) t -> p k t", p=128),
                           reads=dts(y_name, range(16), [tb]), writes=[Ty])
                    sq, Tsq = sqr.next()
                    for q4 in range(4):
                        sc.op("act", OP("activation", out=sq[:, 4 * q4:4 * q4 + 4, :], in_=y[:, 4 * q4:4 * q4 + 4, :], func=AF.Square),
                              reads=[Ty], writes=[Tsq])
                    r, Tr = rstd_from_sq(rr, [sq[:, k, :] for k in range(16)], Tsq, 16, 1.0 / D)
                    for k in range(16):
                        sc.op("dve", OP("scalar_tensor_tensor", out=y[:, k, :], in0=y[:, k, :], scalar=vcol(g_post, l, k), in1=r[:, :], op0=ALU.mult, op1=ALU.mult),
                              reads=[Tr, Tconst], writes=[Ty])
                    for (eng, k0, k1) in (("pool", 0, 5), ("dve", 10, 13), ("pool", 5, 10), ("dve", 13, 16)):
                        sc.op(eng, OP("tensor_tensor", out=x[:, k0:k1, :], in0=x[:, k0:k1, :], in1=y[:, k0:k1, :], op=ALU.add),
                              reads=[Ty], writes=[Tx])
                    sc.dma("sp", Dx, dr[x_out][:, tsl].rearrange("(k p) t -> p k t", p=128), x[:, :, :],
                           reads=[Tx], writes=dts(x_out, range(16), [tb]))

                def stage_b(tb):
                    tsl = slice(tb * 512, (tb + 1) * 512)
                    x, Tx = live.pop(tb)
                    if not g_pre:
                        return
                    sq, Tsq = sqr.next()
                    for q4 in range(4):
                        sc.op("act", OP("activation", out=sq[:, 4 * q4:4 * q4 + 4, :], in_=x[:, 4 * q4:4 * q4 + 4, :], func=AF.Square),
                              reads=[Tx], writes=[Tsq])
                    r, Tr = rstd_from_sq(rr, [sq[:, k, :] for k in range(16)], Tsq, 16, 1.0 / D)
                    h, Th, Dh = hr.next()
                    for k in range(16):
                        sc.op("dve", OP("scalar_tensor_tensor", out=h[:, k, :], in0=x[:, k, :], scalar=vcol(g_pre, l_pre, k), in1=r[:, :], op0=ALU.mult, op1=ALU.mult),
                              reads=[Tr, Tx, Tconst], writes=[Th])
                    sc.dma("sp", Dh, dr[h_out][:, tsl].rearrange("(k p) t -> p k t", p=128), h[:, :, :],
                           reads=[Th], writes=dts(h_out, range(16), [tb]))

                nt = S_tot // 512
                for tb in range(nt):
                    stage_a(tb)
                    if tb >= 1:
                        stage_b(tb - 1)
                stage_b(nt - 1)
                phase_end()

        def linear_phase(in_name, KC, w_ap, jlist, Tn, epi, S_tot=S, BLK=512, setup=None, half_setup=None, in_row0=0, w_reads=None):
            with ExitStack() as st:
                nblk = Tn // BLK
                A = alloc(st, [128, KC, Tn], BF16, "A")
                TA = [T() for _ in range(nblk)]
                DA = [dsem() for _ in range(nblk)]
                Wr = Ring(st, 4, [128, KC, 128], BF16, "W", dma=True)
                env = setup(st) if setup else None
                for th in range(S_tot // Tn):
                    for b in range(nblk):
                        t0 = th * Tn + b * BLK
                        tbs = sorted(set([t0 // 512, (t0 + BLK - 1) // 512]))
                        for k0 in range(0, KC, 16):
                            k1 = min(KC, k0 + 16)
                            sc.dma("sp", DA[b], A[:, k0:k1, b * BLK:(b + 1) * BLK],
                                   dr[in_name][(in_row0 + k0) * 128:(in_row0 + k1) * 128, t0:t0 + BLK].rearrange("(k p) t -> p k t", p=128),
                                   reads=dts(in_name, range(in_row0 + k0, in_row0 + k1), tbs), writes=[TA[b]])
                    if half_setup:
                        half_setup(env, th)
                    for j in jlist:
                        W, TW, DW = Wr.next()
                        for k0 in range(0, KC, 16):
                            k1 = min(KC, k0 + 16)
                            sc.dma("pool", DW, W[:, k0:k1, :], w_ap[k0 * 128:k1 * 128, j * 128:(j + 1) * 128].rearrange("(k p) n -> p k n", p=128),
                                   reads=(w_reads(j) if w_reads else ()), writes=[TW])
                        for b in range(nblk):
                            ps, Tps = next_psum(0, 4)
                            for k in range(KC):
                                sc.op("pe", OP("matmul", ps[:, 0:BLK], lhsT=W[:, k, :], rhs=A[:, k, b * BLK:(b + 1) * BLK], start=(k == 0), stop=(k == KC - 1)),
                                      reads=[TW, TA[b]], writes=[Tps], signal=(k == KC - 1))
                            epi(env, j, th * Tn + b * BLK, ps, Tps)
                phase_end()

        def mk_epi_store(dst, row_of_j, dt, nbuf=3):
            def setup(st):
                return Ring(st, nbuf, [128, 512], dt, "eo", dma=True)

            def epi(ring, j, tok0, ps, Tps, BLK=512):
                o, To, Do = ring.next()
                evac_copy(o[:, :], ps[:, :], Tps, To)
                r = row_of_j(j)
                sc.dma("sp", Do, dr[dst][r * 128:(r + 1) * 128, tok0:tok0 + 512], o[:, :], reads=[To], writes=dts(dst, [r], [tok0 // 512]))
            return setup, epi

        def mk_epi_sigmoid(l):
            def setup(st):
                return Ring(st, 3, [128, 512], BF16, "eo", dma=True)

            def epi(ring, j, tok0, ps, Tps):
                o, To, Do = ring.next()
                sc.op("act", OP("activation", out=o[:, :], in_=ps[:, :], func=AF.Sigmoid, bias=vcol("b_gate", l, j)), reads=[Tconst], writes=[Tps, To])
                sc.dma("sp", Do, dr["gT"][j * 128:(j + 1) * 128, tok0:tok0 + 512], o[:, :], reads=[To], writes=dts("gT", [j], [tok0 // 512]))
            return setup, epi

        def mk_epi_gate(l, goff, m_in, m_out):
            def setup(st):
                return (Ring(st, 3, [128, 512], BF16, "go", dma=True), Ring(st, 3, [128, 512], BF16, "gg", dma=True),
                        Ring(st, 3, [128, 512], BF16, "gm", dma=True), Ring(st, 2, [128, 512], F32, "gt"))

            def epi(env, j, tok0, ps, Tps):
                ro, rg, rm, rt = env
                tb = tok0 // 512
                g, Tg, Dg = rg.next()
                sc.dma("sp", Dg, g[:, :], dr["gT"][(goff + j) * 128:(goff + j + 1) * 128, tok0:tok0 + 512], reads=dts("gT", [goff + j], [tb]), writes=[Tg])
                o, To, Do = ro.next()
                if m_in is None:
                    sc.op("dve", OP("tensor_tensor", out=o[:, :], in0=ps[:, :], in1=g[:, :], op=ALU.mult), reads=[Tg], writes=[Tps, To])
                else:
                    m, Tm, Dm = rm.next()
                    sc.dma("sp", Dm, m[:, :], dr[m_in][j * 128:(j + 1) * 128, tok0:tok0 + 512], reads=dts(m_in, [j], [tb]), writes=[Tm])
                    t, Tt = rt.next()
                    sc.op("dve", OP("tensor_tensor", out=t[:, :], in0=ps[:, :], in1=g[:, :], op=ALU.mult), reads=[Tg], writes=[Tps, Tt])
                    sc.op("dve", OP("tensor_tensor", out=o[:, :], in0=t[:, :], in1=m[:, :], op=ALU.add), reads=[Tt, Tm], writes=[To])
                sc.dma("act", Do, dr[m_out][j * 128:(j + 1) * 128, tok0:tok0 + 512], o[:, :], reads=[To], writes=dts(m_out, [j], [tb]))
            return setup, epi

        def mk_epi_qk(l, Tn):
            def setup(st):
                env = Ctx()
                env.C = alloc(st, [128, Tn], F32, "ropeC")
                env.Sg = alloc(st, [128, Tn], F32, "ropeS")
                env.Tcs = T()
                env.Dcs = dsem()
                env.qg = Ring(st, 2, [128, 512], BF16, "qg")
                env.sq = Ring(st, 2, [128, 512], BF16, "sq")
                env.rr = Ring(st, 2, [128, 512], F32, "rr")
                env.t1 = Ring(st, 2, [128, 512], F32, "t1")
                env.t2 = Ring(st, 2, [128, 512], F32, "t2")
                env.o = Ring(st, 3, [128, 512], BF16, "qo", dma=True)
                env.th = 0
                return env

            def half_setup(env, th):
                env.th = th
                sc.dma("sp", env.Dcs, env.C[:, :], dr["ropeC"][:, th * Tn:(th + 1) * Tn], writes=[env.Tcs])
                sc.dma("sp", env.Dcs, env.Sg[:, :], dr["ropeS"][:, th * Tn:(th + 1) * Tn], writes=[env.Tcs])

            def epi(env, j, tok0, ps, Tps):
                gname = "q_norm_g" if j < 16 else "k_norm_g"
                dst, r = ("qT", j) if j < 16 else ("kT", j - 16)
                c0 = tok0 - env.th * Tn
                qg, Tqg = env.qg.next()
                sq, Tsq = env.sq.next()
                sc.op("act", OP("activation", out=qg[:, :], in_=ps[:, :], func=AF.Copy, scale=vcol(gname, l)), reads=[Tconst], writes=[Tps, Tqg])
                sc.op("act", OP("activation", out=sq[:, :], in_=ps[:, :], func=AF.Square), writes=[Tps, Tsq])
                rs, Trs = rstd_from_sq(env.rr, [sq[:, :]], Tsq, 1, 1.0 / 128)
                ps3, Tps3 = next_psum(4, 8)
                sc.op("pe", OP("matmul", ps3[:, :], lhsT=perm[:, :], rhs=qg[:, :], start=True, stop=True), reads=[Tconst, Tqg], writes=[Tps3])
                t1, Tt1 = env.t1.next()
                t2, Tt2 = env.t2.next()
                sc.op("dve", OP("tensor_tensor", out=t1[:, :], in0=qg[:, :], in1=env.C[:, c0:c0 + 512], op=ALU.mult), reads=[Tqg, env.Tcs], writes=[Tt1])
                sc.op("dve", OP("tensor_tensor", out=t2[:, :], in0=ps3[:, :], in1=env.Sg[:, c0:c0 + 512], op=ALU.mult), reads=[env.Tcs], writes=[Tps3, Tt2])
                sc.op("dve", OP("tensor_tensor", out=t1[:, :], in0=t1[:, :], in1=t2[:, :], op=ALU.add), reads=[Tt2], writes=[Tt1])
                o, To, Do = env.o.next()
                sc.op("dve", OP("tensor_tensor", out=o[:, :], in0=t1[:, :], in1=rs[:, :], op=ALU.mult), reads=[Tt1, Trs], writes=[To])
                sc.dma("sp", Do, dr[dst][r * 128:(r + 1) * 128, tok0:tok0 + 512], o[:, :], reads=[To], writes=dts(dst, [r], [tok0 // 512]))
            return setup, half_setup, epi

        def tokmajor_phase(in_name, KC, jobs, ntok, rhs_loader, dst):
            with ExitStack() as st:
                Ar = Ring(st, 1 if len(jobs) == 1 else 2, [128, KC, ntok], BF16, "A", dma=True)
                tbs = range((ntok + 511) // 512)
                R = alloc(st, [128, KC, 512], BF16, "R")
                TR = T()
                rhs_loader(R, TR)
                ring = Ring(st, 3, [128, 512], BF16, "to", dma=True)
                for in_row0, dst_col0, split in jobs:
                    A, TA, DA = Ar.next()
                    sc.dma("sp", DA, A[:, :, :], dr[in_name][in_row0 * 128:(in_row0 + KC) * 128, 0:ntok].rearrange("(k p) t -> p k t", p=128),
                           reads=dts(in_name, range(in_row0, in_row0 + KC), tbs), writes=[TA])
                    for tt in range(ntok // 128):
                        ps, Tps = next_psum()
                        for k in range(KC):
                            sc.op("pe", OP("matmul", ps[:, :], lhsT=A[:, k, tt * 128:(tt + 1) * 128], rhs=R[:, k, :], start=(k == 0), stop=(k == KC - 1)),
                                  reads=[TA, TR], writes=[Tps], signal=(k == KC - 1))
                        o, To, Do = ring.next()
                        evac_copy(o[:, :], ps[:, :], Tps, To)
                        if split is None:
                            sc.dma("sp", Do, dr[dst][tt * 128:(tt + 1) * 128, dst_col0:dst_col0 + 512], o[:, :], reads=[To],
                                   writes=dts(dst, [("tm", dst_col0)], [tt // 4]))
                        else:
                            sc.dma("sp", Do, dr[dst][tt * 128:(tt + 1) * 128, split:split + 256], o[:, 0:256], reads=[To],
                                   writes=dts(dst, [("ab", split)], [tt // 4]))
                            sc.dma("sp", Do, dr[dst][S + tt * 128:S + (tt + 1) * 128, split:split + 256], o[:, 256:512], reads=[To],
                                   writes=dts(dst, [("ab", split)], [tt // 4]))
                phase_end()

        def attention_phase(q_name, k_name, v_name, o_name, n_kvh, group, nkc):
            nkey = nkc * 128
            with ExitStack() as st:
                Kr = Ring(st, 2, [128, nkey], BF16, "K", dma=True)
                Vr = Ring(st, 2, [128, nkc, 128], BF16, "V", dma=True)
                Qr = Ring(st, 3, [128, 512], BF16, "Q", dma=True)
                Pr = Ring(st, 5, [128, 512], BF16, "P")
                Or = Ring(st, 2, [128, 512], BF16, "O", dma=True)
                Rr = Ring(st, 2, [128, 512], F32, "R")
                ktb = range((nkey + 511) // 512)
                oi = 0
                for kh in range(n_kvh):
                    Kt, TK, DK = Kr.next()
                    sc.dma("sp", DK, Kt[:, :], dr[k_name][kh * 128:(kh + 1) * 128, 0:nkey], reads=dts(k_name, [kh], ktb), writes=[TK])
                    Vt, TV, DV = Vr.next()
                    sc.dma("sp", DV, Vt[:, :, :], dr[v_name][0:nkey, kh * 128:(kh + 1) * 128].rearrange("(c p) d -> p c d", p=128),
                           reads=dts(v_name, [("tm", 0)], ktb), writes=[TV])
                    for hq in range(group):
                        h = kh * group + hq
                        for qb in range(S // 512):
                            Q, TQ, DQ = Qr.next()
                            sc.dma("sp", DQ, Q[:, :], dr[q_name][h * 128:(h + 1) * 128, qb * 512:(qb + 1) * 512], reads=dts(q_name, [h], [qb]), writes=[TQ])
                            pso, Tpso = psum[5 + (oi % 2)], Tpsum[5 + (oi % 2)]
                            pss, Tpss = psum[7], Tpsum[7]
                            oi += 1
                            pend = []

                            def qk(kc):
                                ps, Tps = next_psum(0, 5)
                                sc.op("pe", OP("matmul", ps[:, :], lhsT=Kt[:, kc * 128:(kc + 1) * 128], rhs=Q[:, :], start=True, stop=True),
                                      reads=[TK, TQ], writes=[Tps])
                                P, TP = Pr.next()
                                sc.op("act", OP("activation", out=P[:, :], in_=ps[:, :], func=AF.Exp, scale=float(128 ** -0.5)), writes=[Tps, TP])
                                pend.append((kc, P, TP))

                            def pv():
                                kc, P, TP = pend.pop(0)
                                sc.op("pe", OP("matmul", pso[:, :], lhsT=Vt[:, kc, :], rhs=P[:, :], start=(kc == 0), stop=(kc == nkc - 1)),
                                      reads=[TV, TP], writes=[Tpso], signal=(kc == nkc - 1))
                                sc.op("pe", OP("matmul", pss[:, :], lhsT=ones[:, :], rhs=P[:, :], start=(kc == 0), stop=(kc == nkc - 1)),
                                      reads=[Tconst, TP], writes=[Tpss], signal=(kc == nkc - 1))

                            for kc in range(nkc):
                                qk(kc)
                                if len(pend) > 3:
                                    pv()
                            while pend:
                                pv()
                            R, TR = Rr.next()
                            sc.op("dve", OP("tensor_copy", out=R[:, :], in_=pss[:, :]), writes=[Tpss, TR])
                            sc.op("dve", OP("reciprocal", out=R[:, :], in_=R[:, :]), writes=[TR])
                            O, TO, DO = Or.next()
                            sc.op("dve", OP("tensor_tensor", out=O[:, :], in0=pso[:, :], in1=R[:, :], op=ALU.mult), reads=[TR], writes=[Tpso, TO])
                            sc.dma("sp", DO, dr[o_name][h * 128:(h + 1) * 128, qb * 512:(qb + 1) * 512], O[:, :], reads=[TO], writes=dts(o_name, [h], [qb]))
                phase_end()

        def fourier_phase():
            with ExitStack() as st:
                CS = alloc(st, [128, 2, 512], BF16, "CS")
                TCS, DCS = T(), dsem()
                sc.dma("sp", DCS, CS[:, :, :], dr["cs256"].rearrange("(k p) n -> p k n", p=128), writes=[TCS])
                Ur = Ring(st, 2, [128, 2, S], BF16, "U", dma=True)
                ABr = Ring(st, 2, [128, 32, 512], BF16, "AB")
                Tr_ = Ring(st, 3, [128, 16, 512], BF16, "Tb", dma=True)
                Yo = Ring(st, 3, [128, 512], BF16, "Yo", dma=True)
                for g in range(4):
                    U, TU, DU = Ur.next()
                    sc.dma("sp", DU, U[:, :, :], dr["ufT"][g * 256:(g + 1) * 256, :].rearrange("(k p) t -> p k t", p=128),
                           reads=dts("ufT", [2 * g, 2 * g + 1], range(8)), writes=[TU])
                    AB, TAB = ABr.next()
                    for tt in range(32):
                        ps, Tps = next_psum(0, 4)
                        for c2 in range(2):
                            sc.op("pe", OP("matmul", ps[:, :], lhsT=U[:, c2, tt * 128:(tt + 1) * 128], rhs=CS[:, c2, :], start=(c2 == 0), stop=(c2 == 1)),
                                  reads=[TU, TCS], writes=[Tps], signal=(c2 == 1))
                        evac_copy(AB[:, tt, :], ps[:, :], Tps, TAB)
                    for sb in range(8):
                        acc = [(psum[4 + 2 * (sb % 2) + c], Tpsum[4 + 2 * (sb % 2) + c]) for c in range(2)]
                        step = 0
                        for tab in ("dftC", "dftS"):
                            for hs in range(2):
                                Tb, TTb, DTb = Tr_.next()
                                sc.dma("sp", DTb, Tb[:, :, :], dr[tab][hs * 2048:(hs + 1) * 2048, sb * 512:(sb + 1) * 512].rearrange("(k p) n -> p k n", p=128), writes=[TTb])
                                for k in range(16):
                                    scn = hs * 16 + k
                                    for c in range(2):
                                        off = (0 if tab == "dftC" else 256) + c * 128
                                        first = (step == 0)
                                        last = (tab == "dftS" and hs == 1 and k == 15)
                                        sc.op("pe", OP("matmul", acc[c][0][:, :], lhsT=AB[:, scn, off:off + 128], rhs=Tb[:, k, :], start=first, stop=last),
                                              reads=[TAB, TTb], writes=[acc[c][1]], signal=last)
                                    step += 1
                        for c in range(2):
                            o, To, Do = Yo.next()
                            evac_copy(o[:, :], acc[c][0][:, :], acc[c][1], To)
                            r = 2 * g + c
                            sc.dma("sp", Do, dr["YT"][r * 128:(r + 1) * 128, sb * 512:(sb + 1) * 512], o[:, :], reads=[To], writes=dts("YT", [r], [sb]))
                phase_end()

        def ffn_up_phase(l):
            Tn = 2048
            NC_ = Tn + 2
            w_up = dr["w_up"][l]
            with ExitStack() as st:
                A = alloc(st, [128, 16, NC_], BF16, "A")
                TA, DA = T(), dsem()
                Wr = Ring(st, 4, [128, 16, 128], BF16, "W", dma=True)
                Ur = Ring(st, 3, [128, NC_], F32, "U")
                Cr = Ring(st, 3, [128, Tn], F32, "Cv")
                Gr = Ring(st, 2, [128, Tn], F32, "Gl")
                Or = Ring(st, 2, [128, Tn], BF16, "Ao", dma=True)
                for th in range(2):
                    t_lo = th * Tn - 1
                    if th == 0:
                        sc.op("pool", OP("memset", A[:, :, 0:1], 0.0), writes=[TA])
                        sc.dma("sp", DA, A[:, :, 1:NC_], dr["hT"][:, 0:Tn + 1].rearrange("(k p) t -> p k t", p=128),
                               reads=dts("hT", range(16), range(0, 5)), writes=[TA])
                    else:
                        sc.op("pool", OP("memset", A[:, :, NC_ - 1:NC_], 0.0), writes=[TA])
                        sc.dma("sp", DA, A[:, :, 0:NC_ - 1], dr["hT"][:, t_lo:S].rearrange("(k p) t -> p k t", p=128),
                               reads=dts("hT", range(16), range(3, 8)), writes=[TA])
                    for i in range(NFF):
                        res = []
                        for part in range(2):
                            jc = part * NFF + i
                            W, TW, DW = Wr.next()
                            sc.dma("pool", DW, W[:, :, :], w_up[:, jc * 128:(jc + 1) * 128].rearrange("(k p) n -> p k n", p=128), writes=[TW])
                            U, TU = Ur.next()
                            cv, Tcv = Cr.next()
                            for b in range(5):
                                c0 = b * 512
                                n = 512 if b < 4 else 2
                                ps, Tps = next_psum()
                                for k in range(16):
                                    sc.op("pe", OP("matmul", ps[:, 0:n], lhsT=W[:, k, :], rhs=A[:, k, c0:c0 + n], start=(k == 0), stop=(k == 15)),
                                          reads=[TW, TA], writes=[Tps], signal=(k == 15))
                                sc.op("act", OP("activation", out=U[:, c0:c0 + n], in_=ps[:, 0:n], func=AF.Copy), reads=[Tconst], writes=[Tps, TU])
                            sc.op("act", OP("activation", out=cv[:, :], in_=U[:, 1:Tn + 1], func=AF.Identity, scale=vcol("conv_w1", l, jc), bias=vcol("conv_b", l, jc)),
                                  reads=[TU, Tconst], writes=[Tcv])
                            sc.op("dve", OP("scalar_tensor_tensor", out=cv[:, :], in0=U[:, 0:Tn], scalar=vcol("conv_w0", l, jc), in1=cv[:, :], op0=ALU.mult, op1=ALU.add),
                                  reads=[TU, Tconst], writes=[Tcv])
                            sc.op("dve", OP("scalar_tensor_tensor", out=cv[:, :], in0=U[:, 2:Tn + 2], scalar=vcol("conv_w2", l, jc), in1=cv[:, :], op0=ALU.mult, op1=ALU.add),
                                  reads=[TU, Tconst], writes=[Tcv])
                            res.append((cv, Tcv))
                        gl, Tgl = Gr.next()
                        (cg, Tcg), (cvv, Tcvv) = res
                        sc.op("act", OP("activation", out=gl[:, :], in_=cg[:, :], func=AF.Gelu_apprx_tanh), reads=[Tcg], writes=[Tgl])
                        o, To, Do = Or.next()
                        sc.op("dve", OP("tensor_tensor", out=o[:, :], in0=gl[:, :], in1=cvv[:, :], op=ALU.mult), reads=[Tgl, Tcvv], writes=[To])
                        sc.dma("sp", Do, dr["actT"][i * 128:(i + 1) * 128, th * Tn:(th + 1) * Tn], o[:, :], reads=[To],
                               writes=dts("actT", [i], range(th * 4, th * 4 + 4)))
                phase_end()

        def mem_phase(l):
            with ExitStack() as st:
                x = alloc(st, [128, 16, 256], F32, "mx")
                sq = alloc(st, [128, 16, 256], BF16, "msq")
                h = alloc(st, [128, 16, 256], BF16, "mh")
                r = alloc(st, [128, 256], F32, "mr")
                Tx, Tsq, Th, Tr, Dx = T(), T(), T(), T(), dsem()
                sc.dma("sp", Dx, x[:, :, :], dr["memT"].rearrange("(k p) t -> p k t", p=128), writes=[Tx])
                sc.op("act", OP("activation", out=sq[:, :, :], in_=x[:, :, :], func=AF.Square), reads=[Tx], writes=[Tsq])
                ps, Tps = next_psum()
                for k in range(16):
                    sc.op("pe", OP("matmul", ps[:, 0:256], lhsT=ones[:, :], rhs=sq[:, k, :], start=(k == 0), stop=(k == 15)),
                          reads=[Tconst, Tsq], writes=[Tps], signal=(k == 15))
                sc.op("act", OP("activation", out=r[:, :], in_=ps[:, 0:256], func=AF.Sqrt, scale=1.0 / D, bias=EPS), writes=[Tps, Tr])
                sc.op("dve", OP("reciprocal", out=r[:, :], in_=r[:, :]), writes=[Tr])
                for k in range(16):
                    sc.op("dve", OP("scalar_tensor_tensor", out=h[:, k, :], in0=x[:, k, :], scalar=vcol("mem_norm_g", l, k), in1=r[:, :], op0=ALU.mult, op1=ALU.mult),
                          reads=[Tr, Tx, Tconst], writes=[Th])
                sc.dma("sp", Dx, dr["memnT"].rearrange("(k p) t -> p k t", p=128), h[:, :, :], reads=[Th], writes=dts("memnT", range(16), [0]))
                phase_end()

        phases = []

        def run(name, fn, *a, **kw):
            phases.append(name)
            if stop_after is not None and len(phases) > stop_after:
                return
            if only is not None and name not in only:
                return
            fn(*a, **kw)

        x_cur = "xT"
        run("pre0", post_phase, 0, None, None, x_cur, None, "mix_pre_g", "hT")
        for l in range(n_layers):
            su, hs, ep = mk_epi_qk(l, 2048)
            run("qk", linear_phase, "hT", 16, dr["w_in"][l], list(range(20)), 2048, ep, setup=su, half_setup=hs)
            su, ep = mk_epi_store("ufT", lambda j: j - 24, BF16)
            run("uf", linear_phase, "hT", 16, dr["w_in"][l], list(range(24, 32)), 2048, ep, setup=su)
            su, ep = mk_epi_sigmoid(l)
            run("gate", linear_phase, "hT", 16, dr["w_gate"][l], list(range(32)), 2048, ep, setup=su)
            def vload(R, TR, l=l):
                sc.dma("pool", dsem(), R[:, :, :], dr["w_in"][l][:, 2560:3072].rearrange("(k p) n -> p k n", p=128), writes=[TR])
            run("v", tokmajor_phase, "hT", 16, [(0, 0, None)], S, vload, "vtm")
            run("attn", attention_phase, "qT", "kT", "vtm", "oT", 4, 4, 32)
            def csload(R, TR):
                sc.dma("sp", dsem(), R[:, :, :], dr["cs256"].rearrange("(k p) n -> p k n", p=128), writes=[TR])
            run("fab", tokmajor_phase, "ufT", 2, [(2 * g, 0, g * 256) for g in range(4)], S, csload, "abS")
            su, ep = mk_epi_store("YT", lambda j: j, BF16)
            run("fdft", linear_phase, "dftCS", 64, dr["abS"], list(range(8)), 512, ep, setup=su,
                w_reads=lambda j: dts("abS", [("ab", (j // 2) * 256)], range(8)))
            su, ep = mk_epi_gate(l, 0, None, "m1T")
            run("ao", linear_phase, "oT", 16, dr["w_attn_o"][l], list(range(16)), 2048, ep, setup=su)
            su, ep = mk_epi_gate(l, 16, "m1T", "mT")
            run("fo", linear_phase, "YT", 8, dr["w_four_o"][l], list(range(16)), 2048, ep, setup=su)
            su, ep = mk_epi_store("yT", lambda j: j, F32)
            run("mix", linear_phase, "mT", 16, dr["w_mix_o"][l], list(range(16)), 2048, ep, setup=su)
            x_nxt = "xa"
            run("post1", post_phase, l, "yT", "mix_post_g", x_cur, x_nxt, "xa_pre_g", "hT")
            x_cur = x_nxt
            run("mem", mem_phase, l)
            su, ep = mk_epi_store("qxT", lambda j: j, BF16)
            run("xq", linear_phase, "hT", 16, dr["w_xq"][l], list(range(4)), 2048, ep, setup=su)

            def mk_small_store(dst):
                def setup(st):
                    return Ring(st, 3, [128, 256], BF16, "ko", dma=True)

                def epi(ring, j, tok0, ps, Tps):
                    o, To, Do = ring.next()
                    evac_copy(o[:, :], ps[:, 0:256], Tps, To)
                    sc.dma("sp", Do, dr[dst][j * 128:(j + 1) * 128, 0:256], o[:, :], reads=[To], writes=dts(dst, [j], [0]))
                return setup, epi
            su, ep = mk_small_store("kxT")
            run("xk", linear_phase, "memnT", 16, dr["w_xkv"][l], list(range(4)), 256, ep, S_tot=256, BLK=256, setup=su)

            def vxload(R, TR, l=l):
                sc.dma("pool", dsem(), R[:, :, :], dr["w_xkv"][l][:, 512:1024].rearrange("(k p) n -> p k n", p=128), writes=[TR])
            run("xv", tokmajor_phase, "memnT", 16, [(0, 0, None)], 256, vxload, "vxtm")
            run("xattn", attention_phase, "qxT", "kxT", "vxtm", "oxT", 4, 1, 2)
            su, ep = mk_epi_store("yT", lambda j: j, F32)
            run("xo", linear_phase, "oxT", 4, dr["w_xo"][l], list(range(16)), 4096, ep, setup=su)
            x_nxt = "xb"
            run("post2", post_phase, l, "yT", "xa_post_g", x_cur, x_nxt, "ffn_pre_g", "hT")
            x_cur = x_nxt
            run("ffn_up", ffn_up_phase, l)
            su, ep = mk_epi_store("yT", lambda j: j, F32)
            run("ffn_down", linear_phase, "actT", NFF, dr["w_down"][l], list(range(16)), 1024, ep, setup=su)
            last = (l == n_layers - 1)
            x_nxt = "outT" if last else "xa"
            run("post3", post_phase, l, "yT", "ffn_post_g", x_cur, x_nxt, None if last else "mix_pre_g", "hT", l_pre=l + 1)
            x_cur = x_nxt
    ctx.phases = phases
    return nc, ctx


WNAMES = ["w_in", "w_attn_o", "w_four_o", "w_gate", "w_mix_o", "w_xq", "w_xkv", "w_xo", "w_up", "w_down"]


def make_in_map(inp, b, tabs, vecs):
    m = {"xT": np.ascontiguousarray(np.asarray(inp["x"][b], np.float32).T),
         "memT": np.ascontiguousarray(np.asarray(inp["mem"][b], np.float32).T),
         "vecs": vecs}
    m.update(tabs)
    for w in WNAMES:
        m[w] = np.ascontiguousarray(np.asarray(inp[w], np.float32))
    return m


def kernel(**inputs):
    inp = {k: np.asarray(v) for k, v in inputs.items()}
    tabs = host_tables()
    vecs = pack_vecs(inp)
    nc, _ = build_program(n_layers=L)
    base = make_in_map(inp, 0, tabs, vecs)
    in_maps = []
    for b in range(NCORES):
        m = dict(base)
        m["xT"] = np.ascontiguousarray(np.asarray(inp["x"][b], np.float32).T)
        m["memT"] = np.ascontiguousarray(np.asarray(inp["mem"][b], np.float32).T)
        in_maps.append(m)
    res = run_bass_kernel_spmd(nc, in_maps, core_ids=list(range(NCORES)))
    out = np.empty((NCORES, S, D), np.float32)
    for b in range(NCORES):
        out[b] = np.asarray(res.results[b]["outT"], np.float32).T
    return out
```

```python
import numpy as np
import ml_dtypes
from contextlib import ExitStack
import concourse.bass as bass
import concourse.mybir as mybir
from concourse.bass_utils import run_bass_kernel_spmd

F32 = mybir.dt.float32
BF16 = mybir.dt.bfloat16
AF = mybir.ActivationFunctionType
ALU = mybir.AluOpType

S = 4096
D = 2048
NCORES = 8
SAME_ENGINE_SYNC = True


def OP(name, *a, **k):
    return (name, a, k)


class T:
    __slots__ = ("w", "r")

    def __init__(self):
        self.w = None
        self.r = {}


class Sem:
    __slots__ = ("h", "cnt", "name")

    def __init__(self, h, name):
        self.h = h
        self.cnt = 0
        self.name = name


class Sched:
    ENG = ("pe", "act", "dve", "pool", "sp")

    def __init__(self, nc, stack):
        self.nc = nc
        self.stack = stack
        self.ops = {e: [] for e in self.ENG}
        self.esem = {e: Sem(stack.enter_context(nc.semaphore(f"es_{e}")), e) for e in self.ENG}
        self.seen = {e: {} for e in self.ENG}
        self.dsems = []
        self.nops = 0
        self.limit = 1 << 60

    def dma_sem(self, name):
        s = Sem(self.stack.enter_context(self.nc.semaphore(f"ds_{name}")), name)
        self.dsems.append(s)
        return s

    def _waits(self, eng, reads, writes):
        evs = {}
        for t in reads:
            if t.w is not None:
                s, v = t.w
                if evs.get(s, 0) < v:
                    evs[s] = v
        for t in writes:
            if t.w is not None:
                s, v = t.w
                if evs.get(s, 0) < v:
                    evs[s] = v
            for s, v in t.r.items():
                if evs.get(s, 0) < v:
                    evs[s] = v
        own = self.esem[eng]
        out = []
        seen = self.seen[eng]
        for s, v in evs.items():
            if s is own and (eng == "pe" or not SAME_ENGINE_SYNC):
                continue
            if seen.get(s, 0) >= v:
                continue
            seen[s] = v
            out.append((s.h, v))
        return out

    @staticmethod
    def _commit(ev, reads, writes):
        s, v = ev
        for t in reads:
            if t.r.get(s, 0) < v:
                t.r[s] = v
        for t in writes:
            t.w = ev
            t.r = {}

    def op(self, eng, fn, reads=(), writes=(), signal=True):
        assert signal or eng == "pe"
        if self.nops >= self.limit:
            return
        waits = self._waits(eng, reads, writes)
        es = self.esem[eng]
        ev = (es, es.cnt + 1)
        if signal:
            es.cnt += 1
        self.ops[eng].append((waits, fn, (es.h, 1) if signal else None))
        self._commit(ev, reads, writes)
        self.nops += 1

    def dma(self, q, dsem, out_ap, in_ap, reads=(), writes=()):
        if self.nops >= self.limit:
            return
        waits = self._waits(q, reads, writes)
        dsem.cnt += 16
        ev = (dsem, dsem.cnt)
        self.ops[q].append((waits, OP("dma_start", out=out_ap, in_=in_ap), (dsem.h, 16)))
        self._commit(ev, reads, writes)
        self.nops += 1

    def finish(self):
        waits = []
        for s in self.dsems:
            if s.cnt:
                waits.append((s.h, s.cnt))
        for e in self.ENG:
            if e != "sp" and self.esem[e].cnt:
                waits.append((self.esem[e].h, self.esem[e].cnt))
        self.ops["sp"].append((waits, None, None))

    def emit(self):
        nc = self.nc
        ops = self.ops

        def replay(name, eng):
            for waits, fn, sig in ops[name]:
                for h, v in waits:
                    eng.wait_ge(h, v)
                if fn is None:
                    continue
                inst = getattr(eng, fn[0])(*fn[1], **fn[2])
                if sig is not None:
                    inst.then_inc(sig[0], sig[1])

        self.finish()
        with nc.Block() as block:
            @block.sync
            def _(e):
                replay("sp", e)

            @block.scalar
            def _(e):
                replay("act", e)

            @block.vector
            def _(e):
                replay("dve", e)

            @block.gpsimd
            def _(e):
                replay("pool", e)

            @block.tensor
            def _(e):
                replay("pe", e)
        self.ops = {e: [] for e in self.ENG}


L = 2
DFF = 5632
NFF = DFF // 128
EPS = 1e-6
VEC_SPEC = [
    ("mix_pre_g", 16), ("mix_post_g", 16), ("xa_pre_g", 16), ("xa_post_g", 16),
    ("ffn_pre_g", 16), ("ffn_post_g", 16), ("mem_norm_g", 16), ("b_gate", 32),
    ("q_norm_g", 1), ("k_norm_g", 1), ("conv_w0", 88), ("conv_w1", 88), ("conv_w2", 88), ("conv_b", 88),
]


def vec_layout():
    lay = {}
    c = 0
    for l in range(L):
        for nm, n in VEC_SPEC:
            lay[(nm, l)] = c
            c += n
    return lay, c


def host_tables():
    pos = np.arange(S)
    row = (pos // 64).astype(np.float64)
    col = (pos % 64).astype(np.float64)
    inv = 1.0 / (10000.0 ** (np.arange(0, 64, 2, dtype=np.float32) / 64)).astype(np.float64)
    p = np.arange(128)
    axis = p // 64
    fi = p % 32
    posax = np.where(axis[:, None] == 0, row[None, :], col[None, :])
    ang = (posax.astype(np.float32) * inv[fi][:, None].astype(np.float32)).astype(np.float32)
    ropeC = np.cos(ang).astype(np.float32)
    sgn = np.where((p % 64) < 32, -1.0, 1.0)[:, None]
    ropeS = (np.sin(ang) * sgn).astype(np.float32)
    partner = np.where((p % 64) < 32, p + 32, p - 32)
    perm = np.zeros((128, 128), np.float32)
    perm[partner, p] = 1.0
    ones = np.ones((128, 128), np.float32)
    n = np.arange(S, dtype=np.int64)
    ph = (np.outer(n, n) % S).astype(np.float64) * (2 * np.pi / S)
    dftCS = np.concatenate([np.cos(ph) / 64.0, -np.sin(ph) / 64.0], axis=0).astype(ml_dtypes.bfloat16)
    c = np.arange(256, dtype=np.int64)
    ph2 = (np.outer(c, c) % 256).astype(np.float64) * (2 * np.pi / 256)
    cs256 = np.concatenate([np.cos(ph2) / 16.0, np.sin(ph2) / 16.0], axis=1).astype(ml_dtypes.bfloat16)
    return dict(ropeC=ropeC, ropeS=ropeS, perm=perm.astype(ml_dtypes.bfloat16),
                ones=ones.astype(ml_dtypes.bfloat16), dftCS=dftCS, cs256=cs256)


def pack_vecs(inp):
    lay, nv = vec_layout()
    out = np.zeros((128, nv), np.float32)
    for l in range(L):
        for nm, n in VEC_SPEC:
            if nm.startswith("conv_w"):
                v = inp["conv_w"][l, int(nm[-1])]
            elif nm in ("q_norm_g", "k_norm_g"):
                v = inp[nm][l]
            else:
                v = inp[nm][l]
            c0 = lay[(nm, l)]
            out[:, c0:c0 + n] = np.asarray(v, np.float32).reshape(n, 128).T
    return out


class Ctx:
    pass


def build_program(n_layers=L, debug_outs=(), stop_after=None, only=None, limit=None):
    nc = bass.Bass("TRN2", target_bir_lowering=False)
    ctx = Ctx()
    ctx.nc = nc
    lay, NV = vec_layout()
    dr = {}

    def din(name, shape, dt):
        dr[name] = nc.dram_tensor(name, list(shape), dt, kind="ExternalInput").ap()

    def dscr(name, shape, dt):
        kind = "ExternalOutput" if name in debug_outs else "Internal"
        dr[name] = nc.dram_tensor(name, list(shape), dt, kind=kind).ap()

    din("xT", [D, S], F32)
    din("memT", [D, 256], F32)
    din("vecs", [128, NV], F32)
    din("ropeC", [128, S], F32)
    din("ropeS", [128, S], F32)
    din("perm", [128, 128], BF16)
    din("ones", [128, 128], BF16)
    din("dftCS", [2 * S, S], BF16)
    din("cs256", [256, 512], BF16)
    for nm, shp in [("w_in", [L, D, 4096]), ("w_attn_o", [L, D, D]), ("w_four_o", [L, 1024, D]),
                    ("w_gate", [L, D, 4096]), ("w_mix_o", [L, D, D]), ("w_xq", [L, D, 512]),
                    ("w_xkv", [L, D, 1024]), ("w_xo", [L, 512, D]), ("w_up", [L, D, 2 * DFF]),
                    ("w_down", [L, DFF, D])]:
        din(nm, shp, F32)
    dr["outT"] = nc.dram_tensor("outT", [D, S], F32, kind="ExternalOutput").ap()
    for nm, shp, dt in [("xa", [D, S], F32), ("xb", [D, S], F32), ("hT", [D, S], BF16), ("qT", [D, S], BF16),
                        ("kT", [512, S], BF16), ("vtm", [S, 512], BF16), ("ufT", [1024, S], BF16),
                        ("gT", [4096, S], BF16), ("oT", [D, S], BF16), ("YT", [1024, S], BF16),
                        ("m1T", [D, S], BF16), ("mT", [D, S], BF16), ("yT", [D, S], F32),
                        ("memnT", [D, 256], BF16), ("qxT", [512, S], BF16), ("kxT", [512, 256], BF16),
                        ("vxtm", [256, 512], BF16), ("oxT", [512, S], BF16), ("actT", [DFF, S], BF16), ("abS", [2 * S, 1024], BF16)]:
        dscr(nm, shp, dt)

    dtiles = {}

    def dts(name, rows, tbs):
        out = []
        for r in rows:
            for tb in tbs:
                key = (name, r, tb)
                t = dtiles.get(key)
                if t is None:
                    t = dtiles[key] = T()
                out.append(t)
        return out

    uid = [0]
    with ExitStack() as top:
        sc = Sched(nc, top)
        ctx.sc = sc
        if limit is not None:
            sc.limit = limit
        dpool = [sc.dma_sem(f"p{i}") for i in range(56)]
        dnext = [0]

        def dsem():
            s_ = dpool[dnext[0] % len(dpool)]
            dnext[0] += 1
            return s_

        def alloc(st, shape, dt, nm="t"):
            uid[0] += 1
            return st.enter_context(nc.sbuf_tensor(f"{nm}_{uid[0]}", list(shape), dt))

        vecs = alloc(top, [128, NV], F32, "vecs")
        ones = alloc(top, [128, 128], BF16, "ones")
        perm = alloc(top, [128, 128], BF16, "perm")
        Tconst = T()
        sc.dma("sp", dsem(), vecs[:, :], dr["vecs"], writes=[Tconst])
        sc.dma("sp", dsem(), ones[:, :], dr["ones"], writes=[Tconst])
        sc.dma("sp", dsem(), perm[:, :], dr["perm"], writes=[Tconst])
        psum = [top.enter_context(nc.psum_tensor(f"ps{i}", [128, 512], F32)) for i in range(8)]
        Tpsum = [T() for _ in range(8)]
        pcnt = [0]

        def next_psum(lo=0, hi=8):
            i = lo + pcnt[0] % (hi - lo)
            pcnt[0] += 1
            return psum[i], Tpsum[i]

        def vcol(nm, l, k=0):
            c = lay[(nm, l)] + k
            return vecs[:, c:c + 1]

        class Ring:
            def __init__(self, st, n, shape, dt, nm="r", dma=False):
                self.b = [alloc(st, shape, dt, nm) for _ in range(n)]
                self.t = [T() for _ in range(n)]
                self.d = [dsem() for _ in range(n)] if dma else None
                self.i = 0

            def next(self):
                i = self.i % len(self.b)
                self.i += 1
                return (self.b[i], self.t[i], self.d[i]) if self.d else (self.b[i], self.t[i])

        alt = [0]

        def evac_copy(out_ap, ps, Tps, Tout):
            alt[0] += 1
            if alt[0] % 2:
                sc.op("act", OP("activation", out=out_ap, in_=ps, func=AF.Copy), reads=[Tconst], writes=[Tps, Tout])
            else:
                sc.op("dve", OP("tensor_copy", out=out_ap, in_=ps), writes=[Tps, Tout])

        def phase_end():
            dnext[0] = 0
            sc.emit()

        def rstd_from_sq(st_ring_r, sq_list, Tsq, n_red, scale, recip=True):
            ps, Tps = next_psum()
            n = len(sq_list)
            for i, ap in enumerate(sq_list):
                sc.op("pe", OP("matmul", ps[:, :], lhsT=ones[:, :], rhs=ap, start=(i == 0), stop=(i == n - 1)),
                      reads=[Tconst, Tsq], writes=[Tps], signal=(i == n - 1))
            r, Tr = st_ring_r.next()
            sc.op("act", OP("activation", out=r[:, :], in_=ps[:, :], func=AF.Sqrt, scale=scale, bias=EPS), writes=[Tps, Tr])
            if recip:
                sc.op("dve", OP("reciprocal", out=r[:, :], in_=r[:, :]), writes=[Tr])
            return r, Tr

        def post_phase(l, y_name, g_post, x_in, x_out, g_pre, h_out, l_pre=None, S_tot=S):
            l_pre = l if l_pre is None else l_pre
            with ExitStack() as st:
                xr = Ring(st, 2, [128, 16, 512], F32, "x", dma=True)
                yr = Ring(st, 2, [128, 16, 512], F32, "y", dma=True) if y_name else None
                sqr = Ring(st, 1, [128, 16, 512], BF16, "sq")
                hr = Ring(st, 1, [128, 16, 512], BF16, "h", dma=True)
                rr = Ring(st, 3, [128, 512], F32, "rs")
                live = {}

                def stage_a(tb):
                    tsl = slice(tb * 512, (tb + 1) * 512)
                    x, Tx, Dx = xr.next()
                    sc.dma("sp", Dx, x[:, :, :], dr[x_in][:, tsl].rearrange("(k p) t -> p k t", p=128),
                           reads=dts(x_in, range(16), [tb]), writes=[Tx])
                    live[tb] = (x, Tx)
                    if not y_name:
                        return
                    y, Ty, Dy = yr.next()
                    sc.dma("sp", Dy, y[:, :, :], dr[y_name][:, tsl].rearrange("(k p) t -> p k t", p=128),
                           reads=dts(y_name, range(16), [tb]), writes=[Ty])
                    sq, Tsq = sqr.next()
                    for q4 in range(4):
                        sc.op("act", OP("activation", out=sq[:, 4 * q4:4 * q4 + 4, :], in_=y[:, 4 * q4:4 * q4 + 4, :], func=AF.Square),
                              reads=[Ty], writes=[Tsq])
                    r, Tr = rstd_from_sq(rr, [sq[:, k, :] for k in range(16)], Tsq, 16, 1.0 / D)
                    for k in range(16):
                        sc.op("dve", OP("scalar_tensor_tensor", out=y[:, k, :], in0=y[:, k, :], scalar=vcol(g_post, l, k), in1=r[:, :], op0=ALU.mult, op1=ALU.mult),
                              reads=[Tr, Tconst], writes=[Ty])
                    for (eng, k0, k1) in (("pool", 0, 5), ("dve", 10, 13), ("pool", 5, 10), ("dve", 13, 16)):
                        sc.op(eng, OP("tensor_tensor", out=x[:, k0:k1, :], in0=x[:, k0:k1, :], in1=y[:, k0:k1, :], op=ALU.add),
                              reads=[Ty], writes=[Tx])
                    sc.dma("act", Dx, dr[x_out][:, tsl].rearrange("(k p) t -> p k t", p=128), x[:, :, :],
                           reads=[Tx], writes=dts(x_out, range(16), [tb]))

                def stage_b(tb):
                    tsl = slice(tb * 512, (tb + 1) * 512)
                    x, Tx = live.pop(tb)
                    if not g_pre:
                        return
                    sq, Tsq = sqr.next()
                    for q4 in range(4):
                        sc.op("act", OP("activation", out=sq[:, 4 * q4:4 * q4 + 4, :], in_=x[:, 4 * q4:4 * q4 + 4, :], func=AF.Square),
                              reads=[Tx], writes=[Tsq])
                    r, Tr = rstd_from_sq(rr, [sq[:, k, :] for k in range(16)], Tsq, 16, 1.0 / D)
                    h, Th, Dh = hr.next()
                    for k in range(16):
                        sc.op("dve", OP("scalar_tensor_tensor", out=h[:, k, :], in0=x[:, k, :], scalar=vcol(g_pre, l_pre, k), in1=r[:, :], op0=ALU.mult, op1=ALU.mult),
                              reads=[Tr, Tx, Tconst], writes=[Th])
                    sc.dma("act", Dh, dr[h_out][:, tsl].rearrange("(k p) t -> p k t", p=128), h[:, :, :],
                           reads=[Th], writes=dts(h_out, range(16), [tb]))

                nt = S_tot // 512
                for tb in range(nt):
                    stage_a(tb)
                    if tb >= 1:
                        stage_b(tb - 1)
                stage_b(nt - 1)
                phase_end()

        def linear_phase(in_name, KC, w_ap, jlist, Tn, epi, S_tot=S, BLK=512, setup=None, half_setup=None, in_row0=0, w_reads=None, w_queue="pool"):
            with ExitStack() as st:
                nblk = Tn // BLK
                A = alloc(st, [128, KC, Tn], BF16, "A")
                TA = [T() for _ in range(nblk)]
                DA = [dsem() for _ in range(nblk)]
                Wr = Ring(st, 4, [128, KC, 128], BF16, "W", dma=True)
                env = setup(st) if setup else None
                for th in range(S_tot // Tn):
                    for b in range(nblk):
                        t0 = th * Tn + b * BLK
                        tbs = sorted(set([t0 // 512, (t0 + BLK - 1) // 512]))
                        for k0 in range(0, KC, 16):
                            k1 = min(KC, k0 + 16)
                            sc.dma("sp", DA[b], A[:, k0:k1, b * BLK:(b + 1) * BLK],
                                   dr[in_name][(in_row0 + k0) * 128:(in_row0 + k1) * 128, t0:t0 + BLK].rearrange("(k p) t -> p k t", p=128),
                                   reads=dts(in_name, range(in_row0 + k0, in_row0 + k1), tbs), writes=[TA[b]])
                    if half_setup:
                        half_setup(env, th)
                    for j in jlist:
                        W, TW, DW = Wr.next()
                        for k0 in range(0, KC, 16):
                            k1 = min(KC, k0 + 16)
                            sc.dma(w_queue, DW, W[:, k0:k1, :], w_ap[k0 * 128:k1 * 128, j * 128:(j + 1) * 128].rearrange("(k p) n -> p k n", p=128),
                                   reads=(w_reads(j) if w_reads else ()), writes=[TW])
                        for b in range(nblk):
                            ps, Tps = next_psum(0, 4)
                            for k in range(KC):
                                sc.op("pe", OP("matmul", ps[:, 0:BLK], lhsT=W[:, k, :], rhs=A[:, k, b * BLK:(b + 1) * BLK], start=(k == 0), stop=(k == KC - 1)),
                                      reads=[TW, TA[b]], writes=[Tps], signal=(k == KC - 1))
                            epi(env, j, th * Tn + b * BLK, ps, Tps)
                phase_end()

        def mk_epi_store(dst, row_of_j, dt, nbuf=3):
            def setup(st):
                return Ring(st, nbuf, [128, 512], dt, "eo", dma=True)

            def epi(ring, j, tok0, ps, Tps, BLK=512):
                o, To, Do = ring.next()
                evac_copy(o[:, :], ps[:, :], Tps, To)
                r = row_of_j(j)
                sc.dma("sp", Do, dr[dst][r * 128:(r + 1) * 128, tok0:tok0 + 512], o[:, :], reads=[To], writes=dts(dst, [r], [tok0 // 512]))
            return setup, epi

        def mk_epi_sigmoid(l):
            def setup(st):
                return Ring(st, 3, [128, 512], BF16, "eo", dma=True)

            def epi(ring, j, tok0, ps, Tps):
                o, To, Do = ring.next()
                sc.op("act", OP("activation", out=o[:, :], in_=ps[:, :], func=AF.Sigmoid, bias=vcol("b_gate", l, j)), reads=[Tconst], writes=[Tps, To])
                sc.dma("sp", Do, dr["gT"][j * 128:(j + 1) * 128, tok0:tok0 + 512], o[:, :], reads=[To], writes=dts("gT", [j], [tok0 // 512]))
            return setup, epi

        def mk_epi_gate(l, goff, m_in, m_out):
            def setup(st):
                return (Ring(st, 3, [128, 512], BF16, "go", dma=True), Ring(st, 3, [128, 512], BF16, "gg", dma=True),
                        Ring(st, 3, [128, 512], BF16, "gm", dma=True), Ring(st, 2, [128, 512], F32, "gt"))

            def epi(env, j, tok0, ps, Tps):
                ro, rg, rm, rt = env
                tb = tok0 // 512
                g, Tg, Dg = rg.next()
                sc.dma("sp", Dg, g[:, :], dr["gT"][(goff + j) * 128:(goff + j + 1) * 128, tok0:tok0 + 512], reads=dts("gT", [goff + j], [tb]), writes=[Tg])
                o, To, Do = ro.next()
                if m_in is None:
                    sc.op("dve", OP("tensor_tensor", out=o[:, :], in0=ps[:, :], in1=g[:, :], op=ALU.mult), reads=[Tg], writes=[Tps, To])
                else:
                    m, Tm, Dm = rm.next()
                    sc.dma("sp", Dm, m[:, :], dr[m_in][j * 128:(j + 1) * 128, tok0:tok0 + 512], reads=dts(m_in, [j], [tb]), writes=[Tm])
                    t, Tt = rt.next()
                    sc.op("dve", OP("tensor_tensor", out=t[:, :], in0=ps[:, :], in1=g[:, :], op=ALU.mult), reads=[Tg], writes=[Tps, Tt])
                    sc.op("dve", OP("tensor_tensor", out=o[:, :], in0=t[:, :], in1=m[:, :], op=ALU.add), reads=[Tt, Tm], writes=[To])
                sc.dma("act", Do, dr[m_out][j * 128:(j + 1) * 128, tok0:tok0 + 512], o[:, :], reads=[To], writes=dts(m_out, [j], [tb]))
            return setup, epi

        def mk_epi_qk(l, Tn):
            def setup(st):
                env = Ctx()
                env.C = alloc(st, [128, Tn], F32, "ropeC")
                env.Sg = alloc(st, [128, Tn], F32, "ropeS")
                env.Tcs = T()
                env.Dcs = dsem()
                env.qg = Ring(st, 2, [128, 512], BF16, "qg")
                env.sq = Ring(st, 2, [128, 512], BF16, "sq")
                env.rr = Ring(st, 2, [128, 512], F32, "rr")
                env.t1 = Ring(st, 2, [128, 512], F32, "t1")
                env.t2 = Ring(st, 2, [128, 512], F32, "t2")
                env.o = Ring(st, 3, [128, 512], BF16, "qo", dma=True)
                env.th = 0
                return env

            def half_setup(env, th):
                env.th = th
                sc.dma("sp", env.Dcs, env.C[:, :], dr["ropeC"][:, th * Tn:(th + 1) * Tn], writes=[env.Tcs])
                sc.dma("sp", env.Dcs, env.Sg[:, :], dr["ropeS"][:, th * Tn:(th + 1) * Tn], writes=[env.Tcs])

            def epi(env, j, tok0, ps, Tps):
                gname = "q_norm_g" if j < 16 else "k_norm_g"
                dst, r = ("qT", j) if j < 16 else ("kT", j - 16)
                c0 = tok0 - env.th * Tn
                qg, Tqg = env.qg.next()
                sq, Tsq = env.sq.next()
                sc.op("act", OP("activation", out=qg[:, :], in_=ps[:, :], func=AF.Copy, scale=vcol(gname, l)), reads=[Tconst], writes=[Tps, Tqg])
                sc.op("act", OP("activation", out=sq[:, :], in_=ps[:, :], func=AF.Square), writes=[Tps, Tsq])
                rs, Trs = rstd_from_sq(env.rr, [sq[:, :]], Tsq, 1, 1.0 / 128)
                ps3, Tps3 = next_psum(4, 8)
                sc.op("pe", OP("matmul", ps3[:, :], lhsT=perm[:, :], rhs=qg[:, :], start=True, stop=True), reads=[Tconst, Tqg], writes=[Tps3])
                t1, Tt1 = env.t1.next()
                t2, Tt2 = env.t2.next()
                sc.op("dve", OP("tensor_tensor", out=t1[:, :], in0=qg[:, :], in1=env.C[:, c0:c0 + 512], op=ALU.mult), reads=[Tqg, env.Tcs], writes=[Tt1])
                sc.op("dve", OP("tensor_tensor", out=t2[:, :], in0=ps3[:, :], in1=env.Sg[:, c0:c0 + 512], op=ALU.mult), reads=[env.Tcs], writes=[Tps3, Tt2])
                sc.op("dve", OP("tensor_tensor", out=t1[:, :], in0=t1[:, :], in1=t2[:, :], op=ALU.add), reads=[Tt2], writes=[Tt1])
                o, To, Do = env.o.next()
                sc.op("dve", OP("tensor_tensor", out=o[:, :], in0=t1[:, :], in1=rs[:, :], op=ALU.mult), reads=[Tt1, Trs], writes=[To])
                sc.dma("sp", Do, dr[dst][r * 128:(r + 1) * 128, tok0:tok0 + 512], o[:, :], reads=[To], writes=dts(dst, [r], [tok0 // 512]))
            return setup, half_setup, epi

        def tokmajor_phase(in_name, KC, jobs, ntok, rhs_loader, dst):
            with ExitStack() as st:
                Ar = Ring(st, 1 if len(jobs) == 1 else 2, [128, KC, ntok], BF16, "A", dma=True)
                tbs = range((ntok + 511) // 512)
                R = alloc(st, [128, KC, 512], BF16, "R")
                TR = T()
                rhs_loader(R, TR)
                ring = Ring(st, 3, [128, 512], BF16, "to", dma=True)
                for in_row0, dst_col0, split in jobs:
                    A, TA, DA = Ar.next()
                    sc.dma("sp", DA, A[:, :, :], dr[in_name][in_row0 * 128:(in_row0 + KC) * 128, 0:ntok].rearrange("(k p) t -> p k t", p=128),
                           reads=dts(in_name, range(in_row0, in_row0 + KC), tbs), writes=[TA])
                    for tt in range(ntok // 128):
                        ps, Tps = next_psum()
                        for k in range(KC):
                            sc.op("pe", OP("matmul", ps[:, :], lhsT=A[:, k, tt * 128:(tt + 1) * 128], rhs=R[:, k, :], start=(k == 0), stop=(k == KC - 1)),
                                  reads=[TA, TR], writes=[Tps], signal=(k == KC - 1))
                        o, To, Do = ring.next()
                        evac_copy(o[:, :], ps[:, :], Tps, To)
                        if split is None:
                            sc.dma("sp", Do, dr[dst][tt * 128:(tt + 1) * 128, dst_col0:dst_col0 + 512], o[:, :], reads=[To],
                                   writes=dts(dst, [("tm", dst_col0)], [tt // 4]))
                        else:
                            sc.dma("sp", Do, dr[dst][tt * 128:(tt + 1) * 128, split:split + 256], o[:, 0:256], reads=[To],
                                   writes=dts(dst, [("ab", split)], [tt // 4]))
                            sc.dma("sp", Do, dr[dst][S + tt * 128:S + (tt + 1) * 128, split:split + 256], o[:, 256:512], reads=[To],
                                   writes=dts(dst, [("ab", split)], [tt // 4]))
                phase_end()

        def attention_phase(q_name, k_name, v_name, o_name, n_kvh, group, nkc):
            nkey = nkc * 128
            with ExitStack() as st:
                Kr = Ring(st, 2, [128, nkey], BF16, "K", dma=True)
                Vr = Ring(st, 2, [128, nkc, 128], BF16, "V", dma=True)
                Qr = Ring(st, 3, [128, 512], BF16, "Q", dma=True)
                Pr = Ring(st, 5, [128, 512], BF16, "P")
                Or = Ring(st, 2, [128, 512], BF16, "O", dma=True)
                Rr = Ring(st, 2, [128, 512], F32, "R")
                ktb = range((nkey + 511) // 512)
                oi = 0
                for kh in range(n_kvh):
                    Kt, TK, DK = Kr.next()
                    sc.dma("sp", DK, Kt[:, :], dr[k_name][kh * 128:(kh + 1) * 128, 0:nkey], reads=dts(k_name, [kh], ktb), writes=[TK])
                    Vt, TV, DV = Vr.next()
                    sc.dma("sp", DV, Vt[:, :, :], dr[v_name][0:nkey, kh * 128:(kh + 1) * 128].rearrange("(c p) d -> p c d", p=128),
                           reads=dts(v_name, [("tm", 0)], ktb), writes=[TV])
                    for hq in range(group):
                        h = kh * group + hq
                        for qb in range(S // 512):
                            Q, TQ, DQ = Qr.next()
                            sc.dma("sp", DQ, Q[:, :], dr[q_name][h * 128:(h + 1) * 128, qb * 512:(qb + 1) * 512], reads=dts(q_name, [h], [qb]), writes=[TQ])
                            pso, Tpso = psum[5 + (oi % 2)], Tpsum[5 + (oi % 2)]
                            pss, Tpss = psum[7], Tpsum[7]
                            oi += 1
                            pend = []

                            def qk(kc):
                                ps, Tps = next_psum(0, 5)
                                sc.op("pe", OP("matmul", ps[:, :], lhsT=Kt[:, kc * 128:(kc + 1) * 128], rhs=Q[:, :], start=True, stop=True),
                                      reads=[TK, TQ], writes=[Tps])
                                P, TP = Pr.next()
                                sc.op("act", OP("activation", out=P[:, :], in_=ps[:, :], func=AF.Exp, scale=float(128 ** -0.5)), writes=[Tps, TP])
                                pend.append((kc, P, TP))

                            def pv():
                                kc, P, TP = pend.pop(0)
                                sc.op("pe", OP("matmul", pso[:, :], lhsT=Vt[:, kc, :], rhs=P[:, :], start=(kc == 0), stop=(kc == nkc - 1)),
                                      reads=[TV, TP], writes=[Tpso], signal=(kc == nkc - 1))
                                sc.op("pe", OP("matmul", pss[:, :], lhsT=ones[:, :], rhs=P[:, :], start=(kc == 0), stop=(kc == nkc - 1)),
                                      reads=[Tconst, TP], writes=[Tpss], signal=(kc == nkc - 1))

                            for kc in range(nkc):
                                qk(kc)
                                if len(pend) > 3:
                                    pv()
                            while pend:
                                pv()
                            R, TR = Rr.next()
                            sc.op("dve", OP("tensor_copy", out=R[:, :], in_=pss[:, :]), writes=[Tpss, TR])
                            sc.op("dve", OP("reciprocal", out=R[:, :], in_=R[:, :]), writes=[TR])
                            O, TO, DO = Or.next()
                            sc.op("dve", OP("tensor_tensor", out=O[:, :], in0=pso[:, :], in1=R[:, :], op=ALU.mult), reads=[TR], writes=[Tpso, TO])
                            sc.dma("sp", DO, dr[o_name][h * 128:(h + 1) * 128, qb * 512:(qb + 1) * 512], O[:, :], reads=[TO], writes=dts(o_name, [h], [qb]))
                phase_end()

        def fourier_phase():
            with ExitStack() as st:
                CS = alloc(st, [128, 2, 512], BF16, "CS")
                TCS, DCS = T(), dsem()
                sc.dma("sp", DCS, CS[:, :, :], dr["cs256"].rearrange("(k p) n -> p k n", p=128), writes=[TCS])
                Ur = Ring(st, 2, [128, 2, S], BF16, "U", dma=True)
                ABr = Ring(st, 2, [128, 32, 512], BF16, "AB")
                Tr_ = Ring(st, 3, [128, 16, 512], BF16, "Tb", dma=True)
                Yo = Ring(st, 3, [128, 512], BF16, "Yo", dma=True)
                for g in range(4):
                    U, TU, DU = Ur.next()
                    sc.dma("sp", DU, U[:, :, :], dr["ufT"][g * 256:(g + 1) * 256, :].rearrange("(k p) t -> p k t", p=128),
                           reads=dts("ufT", [2 * g, 2 * g + 1], range(8)), writes=[TU])
                    AB, TAB = ABr.next()
                    for tt in range(32):
                        ps, Tps = next_psum(0, 4)
                        for c2 in range(2):
                            sc.op("pe", OP("matmul", ps[:, :], lhsT=U[:, c2, tt * 128:(tt + 1) * 128], rhs=CS[:, c2, :], start=(c2 == 0), stop=(c2 == 1)),
                                  reads=[TU, TCS], writes=[Tps], signal=(c2 == 1))
                        evac_copy(AB[:, tt, :], ps[:, :], Tps, TAB)
                    for sb in range(8):
                        acc = [(psum[4 + 2 * (sb % 2) + c], Tpsum[4 + 2 * (sb % 2) + c]) for c in range(2)]
                        step = 0
                        for tab in ("dftC", "dftS"):
                            for hs in range(2):
                                Tb, TTb, DTb = Tr_.next()
                                sc.dma("sp", DTb, Tb[:, :, :], dr[tab][hs * 2048:(hs + 1) * 2048, sb * 512:(sb + 1) * 512].rearrange("(k p) n -> p k n", p=128), writes=[TTb])
                                for k in range(16):
                                    scn = hs * 16 + k
                                    for c in range(2):
                                        off = (0 if tab == "dftC" else 256) + c * 128
                                        first = (step == 0)
                                        last = (tab == "dftS" and hs == 1 and k == 15)
                                        sc.op("pe", OP("matmul", acc[c][0][:, :], lhsT=AB[:, scn, off:off + 128], rhs=Tb[:, k, :], start=first, stop=last),
                                              reads=[TAB, TTb], writes=[acc[c][1]], signal=last)
                                    step += 1
                        for c in range(2):
                            o, To, Do = Yo.next()
                            evac_copy(o[:, :], acc[c][0][:, :], acc[c][1], To)
                            r = 2 * g + c
                            sc.dma("sp", Do, dr["YT"][r * 128:(r + 1) * 128, sb * 512:(sb + 1) * 512], o[:, :], reads=[To], writes=dts("YT", [r], [sb]))
                phase_end()

        def ffn_up_phase(l):
            Tn = 2048
            NC_ = Tn + 2
            w_up = dr["w_up"][l]
            with ExitStack() as st:
                A = alloc(st, [128, 16, NC_], BF16, "A")
                TA, DA = T(), dsem()
                Wr = Ring(st, 4, [128, 16, 128], BF16, "W", dma=True)
                Ur = Ring(st, 3, [128, NC_], F32, "U")
                Cr = Ring(st, 3, [128, Tn], F32, "Cv")
                Gr = Ring(st, 2, [128, Tn], F32, "Gl")
                Or = Ring(st, 2, [128, Tn], BF16, "Ao", dma=True)
                for th in range(2):
                    t_lo = th * Tn - 1
                    if th == 0:
                        sc.op("pool", OP("memset", A[:, :, 0:1], 0.0), writes=[TA])
                        sc.dma("sp", DA, A[:, :, 1:NC_], dr["hT"][:, 0:Tn + 1].rearrange("(k p) t -> p k t", p=128),
                               reads=dts("hT", range(16), range(0, 5)), writes=[TA])
                    else:
                        sc.op("pool", OP("memset", A[:, :, NC_ - 1:NC_], 0.0), writes=[TA])
                        sc.dma("sp", DA, A[:, :, 0:NC_ - 1], dr["hT"][:, t_lo:S].rearrange("(k p) t -> p k t", p=128),
                               reads=dts("hT", range(16), range(3, 8)), writes=[TA])
                    for i in range(NFF):
                        res = []
                        for part in range(2):
                            jc = part * NFF + i
                            W, TW, DW = Wr.next()
                            sc.dma("pool", DW, W[:, :, :], w_up[:, jc * 128:(jc + 1) * 128].rearrange("(k p) n -> p k n", p=128), writes=[TW])
                            U, TU = Ur.next()
                            cv, Tcv = Cr.next()
                            for b in range(5):
                                c0 = b * 512
                                n = 512 if b < 4 else 2
                                ps, Tps = next_psum()
                                for k in range(16):
                                    sc.op("pe", OP("matmul", ps[:, 0:n], lhsT=W[:, k, :], rhs=A[:, k, c0:c0 + n], start=(k == 0), stop=(k == 15)),
                                          reads=[TW, TA], writes=[Tps], signal=(k == 15))
                                sc.op("act", OP("activation", out=U[:, c0:c0 + n], in_=ps[:, 0:n], func=AF.Copy), reads=[Tconst], writes=[Tps, TU])
                            sc.op("act", OP("activation", out=cv[:, :], in_=U[:, 1:Tn + 1], func=AF.Identity, scale=vcol("conv_w1", l, jc), bias=vcol("conv_b", l, jc)),
                                  reads=[TU, Tconst], writes=[Tcv])
                            sc.op("dve", OP("scalar_tensor_tensor", out=cv[:, :], in0=U[:, 0:Tn], scalar=vcol("conv_w0", l, jc), in1=cv[:, :], op0=ALU.mult, op1=ALU.add),
                                  reads=[TU, Tconst], writes=[Tcv])
                            sc.op("dve", OP("scalar_tensor_tensor", out=cv[:, :], in0=U[:, 2:Tn + 2], scalar=vcol("conv_w2", l, jc), in1=cv[:, :], op0=ALU.mult, op1=ALU.add),
                                  reads=[TU, Tconst], writes=[Tcv])
                            res.append((cv, Tcv))
                        gl, Tgl = Gr.next()
                        (cg, Tcg), (cvv, Tcvv) = res
                        sc.op("act", OP("activation", out=gl[:, :], in_=cg[:, :], func=AF.Gelu_apprx_tanh), reads=[Tcg], writes=[Tgl])
                        o, To, Do = Or.next()
                        sc.op("dve", OP("tensor_tensor", out=o[:, :], in0=gl[:, :], in1=cvv[:, :], op=ALU.mult), reads=[Tgl, Tcvv], writes=[To])
                        sc.dma("sp", Do, dr["actT"][i * 128:(i + 1) * 128, th * Tn:(th + 1) * Tn], o[:, :], reads=[To],
                               writes=dts("actT", [i], range(th * 4, th * 4 + 4)))
                phase_end()

        def mem_phase(l):
            with ExitStack() as st:
                x = alloc(st, [128, 16, 256], F32, "mx")
                sq = alloc(st, [128, 16, 256], BF16, "msq")
                h = alloc(st, [128, 16, 256], BF16, "mh")
                r = alloc(st, [128, 256], F32, "mr")
                Tx, Tsq, Th, Tr, Dx = T(), T(), T(), T(), dsem()
                sc.dma("sp", Dx, x[:, :, :], dr["memT"].rearrange("(k p) t -> p k t", p=128), writes=[Tx])
                sc.op("act", OP("activation", out=sq[:, :, :], in_=x[:, :, :], func=AF.Square), reads=[Tx], writes=[Tsq])
                ps, Tps = next_psum()
                for k in range(16):
                    sc.op("pe", OP("matmul", ps[:, 0:256], lhsT=ones[:, :], rhs=sq[:, k, :], start=(k == 0), stop=(k == 15)),
                          reads=[Tconst, Tsq], writes=[Tps], signal=(k == 15))
                sc.op("act", OP("activation", out=r[:, :], in_=ps[:, 0:256], func=AF.Sqrt, scale=1.0 / D, bias=EPS), writes=[Tps, Tr])
                sc.op("dve", OP("reciprocal", out=r[:, :], in_=r[:, :]), writes=[Tr])
                for k in range(16):
                    sc.op("dve", OP("scalar_tensor_tensor", out=h[:, k, :], in0=x[:, k, :], scalar=vcol("mem_norm_g", l, k), in1=r[:, :], op0=ALU.mult, op1=ALU.mult),
                          reads=[Tr, Tx, Tconst], writes=[Th])
                sc.dma("sp", Dx, dr["memnT"].rearrange("(k p) t -> p k t", p=128), h[:, :, :], reads=[Th], writes=dts("memnT", range(16), [0]))
                phase_end()

        phases = []

        def run(name, fn, *a, **kw):
            phases.append(name)
            if stop_after is not None and len(phases) > stop_after:
                return
            if only is not None and name not in only:
                return
            fn(*a, **kw)

        x_cur = "xT"
        run("pre0", post_phase, 0, None, None, x_cur, None, "mix_pre_g", "hT")
        for l in range(n_layers):
            su, hs, ep = mk_epi_qk(l, 2048)
            run("qk", linear_phase, "hT", 16, dr["w_in"][l], list(range(20)), 2048, ep, setup=su, half_setup=hs)
            su, ep = mk_epi_store("ufT", lambda j: j - 24, BF16)
            run("uf", linear_phase, "hT", 16, dr["w_in"][l], list(range(24, 32)), 2048, ep, setup=su)
            su, ep = mk_epi_sigmoid(l)
            run("gate", linear_phase, "hT", 16, dr["w_gate"][l], list(range(32)), 2048, ep, setup=su)
            def vload(R, TR, l=l):
                sc.dma("pool", dsem(), R[:, :, :], dr["w_in"][l][:, 2560:3072].rearrange("(k p) n -> p k n", p=128), writes=[TR])
            run("v", tokmajor_phase, "hT", 16, [(0, 0, None)], S, vload, "vtm")
            run("attn", attention_phase, "qT", "kT", "vtm", "oT", 4, 4, 32)
            def csload(R, TR):
                sc.dma("sp", dsem(), R[:, :, :], dr["cs256"].rearrange("(k p) n -> p k n", p=128), writes=[TR])
            run("fab", tokmajor_phase, "ufT", 2, [(2 * g, 0, g * 256) for g in range(4)], S, csload, "abS")
            su, ep = mk_epi_store("YT", lambda j: j, BF16)
            run("fdft", linear_phase, "dftCS", 64, dr["abS"], list(range(8)), 512, ep, setup=su,
                w_reads=lambda j: dts("abS", [("ab", (j // 2) * 256)], range(8)), w_queue="act")
            su, ep = mk_epi_gate(l, 0, None, "m1T")
            run("ao", linear_phase, "oT", 16, dr["w_attn_o"][l], list(range(16)), 2048, ep, setup=su)
            su, ep = mk_epi_gate(l, 16, "m1T", "mT")
            run("fo", linear_phase, "YT", 8, dr["w_four_o"][l], list(range(16)), 2048, ep, setup=su)
            su, ep = mk_epi_store("yT", lambda j: j, F32)
            run("mix", linear_phase, "mT", 16, dr["w_mix_o"][l], list(range(16)), 2048, ep, setup=su)
            x_nxt = "xa"
            run("post1", post_phase, l, "yT", "mix_post_g", x_cur, x_nxt, "xa_pre_g", "hT")
            x_cur = x_nxt
            run("mem", mem_phase, l)
            su, ep = mk_epi_store("qxT", lambda j: j, BF16)
            run("xq", linear_phase, "hT", 16, dr["w_xq"][l], list(range(4)), 2048, ep, setup=su)

            def mk_small_store(dst):
                def setup(st):
                    return Ring(st, 3, [128, 256], BF16, "ko", dma=True)

                def epi(ring, j, tok0, ps, Tps):
                    o, To, Do = ring.next()
                    evac_copy(o[:, :], ps[:, 0:256], Tps, To)
                    sc.dma("sp", Do, dr[dst][j * 128:(j + 1) * 128, 0:256], o[:, :], reads=[To], writes=dts(dst, [j], [0]))
                return setup, epi
            su, ep = mk_small_store("kxT")
            run("xk", linear_phase, "memnT", 16, dr["w_xkv"][l], list(range(4)), 256, ep, S_tot=256, BLK=256, setup=su)

            def vxload(R, TR, l=l):
                sc.dma("pool", dsem(), R[:, :, :], dr["w_xkv"][l][:, 512:1024].rearrange("(k p) n -> p k n", p=128), writes=[TR])
            run("xv", tokmajor_phase, "memnT", 16, [(0, 0, None)], 256, vxload, "vxtm")
            run("xattn", attention_phase, "qxT", "kxT", "vxtm", "oxT", 4, 1, 2)
            su, ep = mk_epi_store("yT", lambda j: j, F32)
            run("xo", linear_phase, "oxT", 4, dr["w_xo"][l], list(range(16)), 4096, ep, setup=su)
            x_nxt = "xb"
            run("post2", post_phase, l, "yT", "xa_post_g", x_cur, x_nxt, "ffn_pre_g", "hT")
            x_cur = x_nxt
            run("ffn_up", ffn_up_phase, l)
            su, ep = mk_epi_store("yT", lambda j: j, F32)
            run("ffn_down", linear_phase, "actT", NFF, dr["w_down"][l], list(range(16)), 1024, ep, setup=su)
            last = (l == n_layers - 1)
            x_nxt = "outT" if last else "xa"
            run("post3", post_phase, l, "yT", "ffn_post_g", x_cur, x_nxt, None if last else "mix_pre_g", "hT", l_pre=l + 1)
            x_cur = x_nxt
    ctx.phases = phases
    return nc, ctx


WNAMES = ["w_in", "w_attn_o", "w_four_o", "w_gate", "w_mix_o", "w_xq", "w_xkv", "w_xo", "w_up", "w_down"]


def make_in_map(inp, b, tabs, vecs):
    m = {"xT": np.ascontiguousarray(np.asarray(inp["x"][b], np.float32).T),
         "memT": np.ascontiguousarray(np.asarray(inp["mem"][b], np.float32).T),
         "vecs": vecs}
    m.update(tabs)
    for w in WNAMES:
        m[w] = np.ascontiguousarray(np.asarray(inp[w], np.float32))
    return m


def kernel(**inputs):
    inp = {k: np.asarray(v) for k, v in inputs.items()}
    tabs = host_tables()
    vecs = pack_vecs(inp)
    nc, _ = build_program(n_layers=L)
    base = make_in_map(inp, 0, tabs, vecs)
    in_maps = []
    for b in range(NCORES):
        m = dict(base)
        m["xT"] = np.ascontiguousarray(np.asarray(inp["x"][b], np.float32).T)
        m["memT"] = np.ascontiguousarray(np.asarray(inp["mem"][b], np.float32).T)
        in_maps.append(m)
    res = run_bass_kernel_spmd(nc, in_maps, core_ids=list(range(NCORES)))
    out = np.empty((NCORES, S, D), np.float32)
    for b in range(NCORES):
        out[b] = np.asarray(res.results[b]["outT"], np.float32).T
    return out
```

```python
import numpy as np
import ml_dtypes
from contextlib import ExitStack
import concourse.bass as bass
import concourse.mybir as mybir
from concourse.bass_utils import run_bass_kernel_spmd

F32 = mybir.dt.float32
BF16 = mybir.dt.bfloat16
AF = mybir.ActivationFunctionType
ALU = mybir.AluOpType

S = 4096
D = 2048
NCORES = 8
SAME_ENGINE_SYNC = True


def OP(name, *a, **k):
    return (name, a, k)


class T:
    __slots__ = ("w", "r")

    def __init__(self):
        self.w = None
        self.r = {}


class Sem:
    __slots__ = ("h", "cnt", "name")

    def __init__(self, h, name):
        self.h = h
        self.cnt = 0
        self.name = name


class Sched:
    ENG = ("pe", "act", "dve", "pool", "sp")

    def __init__(self, nc, stack):
        self.nc = nc
        self.stack = stack
        self.ops = {e: [] for e in self.ENG}
        self.esem = {e: Sem(stack.enter_context(nc.semaphore(f"es_{e}")), e) for e in self.ENG}
        self.seen = {e: {} for e in self.ENG}
        self.dsems = []
        self.nops = 0
        self.limit = 1 << 60

    def dma_sem(self, name):
        s = Sem(self.stack.enter_context(self.nc.semaphore(f"ds_{name}")), name)
        self.dsems.append(s)
        return s

    def _waits(self, eng, reads, writes):
        evs = {}
        for t in reads:
            if t.w is not None:
                s, v = t.w
                if evs.get(s, 0) < v:
                    evs[s] = v
        for t in writes:
            if t.w is not None:
                s, v = t.w
                if evs.get(s, 0) < v:
                    evs[s] = v
            for s, v in t.r.items():
                if evs.get(s, 0) < v:
                    evs[s] = v
        own = self.esem[eng]
        out = []
        seen = self.seen[eng]
        for s, v in evs.items():
            if s is own and (eng == "pe" or not SAME_ENGINE_SYNC):
                continue
            if seen.get(s, 0) >= v:
                continue
            seen[s] = v
            out.append((s.h, v))
        return out

    @staticmethod
    def _commit(ev, reads, writes):
        s, v = ev
        for t in reads:
            if t.r.get(s, 0) < v:
                t.r[s] = v
        for t in writes:
            t.w = ev
            t.r = {}

    def op(self, eng, fn, reads=(), writes=(), signal=True):
        assert signal or eng == "pe"
        if self.nops >= self.limit:
            return
        waits = self._waits(eng, reads, writes)
        es = self.esem[eng]
        ev = (es, es.cnt + 1)
        if signal:
            es.cnt += 1
        self.ops[eng].append((waits, fn, (es.h, 1) if signal else None))
        self._commit(ev, reads, writes)
        self.nops += 1

    def dma(self, q, dsem, out_ap, in_ap, reads=(), writes=()):
        if self.nops >= self.limit:
            return
        waits = self._waits(q, reads, writes)
        dsem.cnt += 16
        ev = (dsem, dsem.cnt)
        self.ops[q].append((waits, OP("dma_start", out=out_ap, in_=in_ap), (dsem.h, 16)))
        self._commit(ev, reads, writes)
        self.nops += 1

    def finish(self):
        waits = []
        for s in self.dsems:
            if s.cnt:
                waits.append((s.h, s.cnt))
        for e in self.ENG:
            if e != "sp" and self.esem[e].cnt:
                waits.append((self.esem[e].h, self.esem[e].cnt))
        self.ops["sp"].append((waits, None, None))

    def emit(self):
        nc = self.nc
        ops = self.ops

        def replay(name, eng):
            for waits, fn, sig in ops[name]:
                for h, v in waits:
                    eng.wait_ge(h, v)
                if fn is None:
                    continue
                inst = getattr(eng, fn[0])(*fn[1], **fn[2])
                if sig is not None:
                    inst.then_inc(sig[0], sig[1])

        self.finish()
        with nc.Block() as block:
            @block.sync
            def _(e):
                replay("sp", e)

            @block.scalar
            def _(e):
                replay("act", e)

            @block.vector
            def _(e):
                replay("dve", e)

            @block.gpsimd
            def _(e):
                replay("pool", e)

            @block.tensor
            def _(e):
                replay("pe", e)
        self.ops = {e: [] for e in self.ENG}


L = 2
DFF = 5632
NFF = DFF // 128
EPS = 1e-6
VEC_SPEC = [
    ("mix_pre_g", 16), ("mix_post_g", 16), ("xa_pre_g", 16), ("xa_post_g", 16),
    ("ffn_pre_g", 16), ("ffn_post_g", 16), ("mem_norm_g", 16), ("b_gate", 32),
    ("q_norm_g", 1), ("k_norm_g", 1), ("conv_w0", 88), ("conv_w1", 88), ("conv_w2", 88), ("conv_b", 88),
]


def vec_layout():
    lay = {}
    c = 0
    for l in range(L):
        for nm, n in VEC_SPEC:
            lay[(nm, l)] = c
            c += n
    return lay, c


def host_tables():
    pos = np.arange(S)
    row = (pos // 64).astype(np.float64)
    col = (pos % 64).astype(np.float64)
    inv = 1.0 / (10000.0 ** (np.arange(0, 64, 2, dtype=np.float32) / 64)).astype(np.float64)
    p = np.arange(128)
    axis = p // 64
    fi = p % 32
    posax = np.where(axis[:, None] == 0, row[None, :], col[None, :])
    ang = (posax.astype(np.float32) * inv[fi][:, None].astype(np.float32)).astype(np.float32)
    ropeC = np.cos(ang).astype(np.float32)
    sgn = np.where((p % 64) < 32, -1.0, 1.0)[:, None]
    ropeS = (np.sin(ang) * sgn).astype(np.float32)
    partner = np.where((p % 64) < 32, p + 32, p - 32)
    perm = np.zeros((128, 128), np.float32)
    perm[partner, p] = 1.0
    ones = np.ones((128, 128), np.float32)
    n = np.arange(S, dtype=np.int64)
    ph = (np.outer(n, n) % S).astype(np.float64) * (2 * np.pi / S)
    dftCS = np.concatenate([np.cos(ph) / 64.0, -np.sin(ph) / 64.0], axis=0).astype(ml_dtypes.bfloat16)
    c = np.arange(256, dtype=np.int64)
    ph2 = (np.outer(c, c) % 256).astype(np.float64) * (2 * np.pi / 256)
    cs256 = np.concatenate([np.cos(ph2) / 16.0, np.sin(ph2) / 16.0], axis=1).astype(ml_dtypes.bfloat16)
    return dict(ropeC=ropeC, ropeS=ropeS, perm=perm.astype(ml_dtypes.bfloat16),
                ones=ones.astype(ml_dtypes.bfloat16), dftCS=dftCS, cs256=cs256)


def pack_vecs(inp):
    lay, nv = vec_layout()
    out = np.zeros((128, nv), np.float32)
    for l in range(L):
        for nm, n in VEC_SPEC:
            if nm.startswith("conv_w"):
                v = inp["conv_w"][l, int(nm[-1])]
            elif nm in ("q_norm_g", "k_norm_g"):
                v = inp[nm][l]
            else:
                v = inp[nm][l]
            c0 = lay[(nm, l)]
            out[:, c0:c0 + n] = np.asarray(v, np.float32).reshape(n, 128).T
    return out


class Ctx:
    pass


def build_program(n_layers=L, debug_outs=(), stop_after=None, only=None, limit=None):
    nc = bass.Bass("TRN2", target_bir_lowering=False)
    ctx = Ctx()
    ctx.nc = nc
    lay, NV = vec_layout()
    dr = {}

    def din(name, shape, dt):
        dr[name] = nc.dram_tensor(name, list(shape), dt, kind="ExternalInput").ap()

    def dscr(name, shape, dt):
        kind = "ExternalOutput" if name in debug_outs else "Internal"
        dr[name] = nc.dram_tensor(name, list(shape), dt, kind=kind).ap()

    din("xT", [D, S], F32)
    din("memT", [D, 256], F32)
    din("vecs", [128, NV], F32)
    din("ropeC", [128, S], F32)
    din("ropeS", [128, S], F32)
    din("perm", [128, 128], BF16)
    din("ones", [128, 128], BF16)
    din("dftCS", [2 * S, S], BF16)
    din("cs256", [256, 512], BF16)
    for nm, shp in [("w_in", [L, D, 4096]), ("w_attn_o", [L, D, D]), ("w_four_o", [L, 1024, D]),
                    ("w_gate", [L, D, 4096]), ("w_mix_o", [L, D, D]), ("w_xq", [L, D, 512]),
                    ("w_xkv", [L, D, 1024]), ("w_xo", [L, 512, D]), ("w_up", [L, D, 2 * DFF]),
                    ("w_down", [L, DFF, D])]:
        din(nm, shp, F32)
    dr["outT"] = nc.dram_tensor("outT", [D, S], F32, kind="ExternalOutput").ap()
    for nm, shp, dt in [("xa", [D, S], F32), ("xb", [D, S], F32), ("hT", [D, S], BF16), ("qT", [D, S], BF16),
                        ("kT", [512, S], BF16), ("vtm", [S, 512], BF16), ("ufT", [1024, S], BF16),
                        ("gT", [4096, S], BF16), ("oT", [D, S], BF16), ("YT", [1024, S], BF16),
                        ("m1T", [D, S], BF16), ("mT", [D, S], BF16), ("yT", [D, S], F32),
                        ("memnT", [D, 256], BF16), ("qxT", [512, S], BF16), ("kxT", [512, 256], BF16),
                        ("vxtm", [256, 512], BF16), ("oxT", [512, S], BF16), ("actT", [DFF, S], BF16), ("abS", [2 * S, 1024], BF16)]:
        dscr(nm, shp, dt)

    dtiles = {}

    def dts(name, rows, tbs):
        out = []
        for r in rows:
            for tb in tbs:
                key = (name, r, tb)
                t = dtiles.get(key)
                if t is None:
                    t = dtiles[key] = T()
                out.append(t)
        return out

    uid = [0]
    with ExitStack() as top:
        sc = Sched(nc, top)
        ctx.sc = sc
        if limit is not None:
            sc.limit = limit
        dpool = [sc.dma_sem(f"p{i}") for i in range(56)]
        dnext = [0]

        def dsem():
            s_ = dpool[dnext[0] % len(dpool)]
            dnext[0] += 1
            return s_

        def alloc(st, shape, dt, nm="t"):
            uid[0] += 1
            return st.enter_context(nc.sbuf_tensor(f"{nm}_{uid[0]}", list(shape), dt))

        vecs = alloc(top, [128, NV], F32, "vecs")
        ones = alloc(top, [128, 128], BF16, "ones")
        perm = alloc(top, [128, 128], BF16, "perm")
        Tconst = T()
        sc.dma("sp", dsem(), vecs[:, :], dr["vecs"], writes=[Tconst])
        sc.dma("sp", dsem(), ones[:, :], dr["ones"], writes=[Tconst])
        sc.dma("sp", dsem(), perm[:, :], dr["perm"], writes=[Tconst])
        psum = [top.enter_context(nc.psum_tensor(f"ps{i}", [128, 512], F32)) for i in range(8)]
        Tpsum = [T() for _ in range(8)]
        pcnt = [0]

        def next_psum(lo=0, hi=8):
            i = lo + pcnt[0] % (hi - lo)
            pcnt[0] += 1
            return psum[i], Tpsum[i]

        def vcol(nm, l, k=0):
            c = lay[(nm, l)] + k
            return vecs[:, c:c + 1]

        class Ring:
            def __init__(self, st, n, shape, dt, nm="r", dma=False):
                self.b = [alloc(st, shape, dt, nm) for _ in range(n)]
                self.t = [T() for _ in range(n)]
                self.d = [dsem() for _ in range(n)] if dma else None
                self.i = 0

            def next(self):
                i = self.i % len(self.b)
                self.i += 1
                return (self.b[i], self.t[i], self.d[i]) if self.d else (self.b[i], self.t[i])

        alt = [0]

        def evac_copy(out_ap, ps, Tps, Tout):
            alt[0] += 1
            if alt[0] % 2:
                sc.op("act", OP("activation", out=out_ap, in_=ps, func=AF.Copy), reads=[Tconst], writes=[Tps, Tout])
            else:
                sc.op("dve", OP("tensor_copy", out=out_ap, in_=ps), writes=[Tps, Tout])

        def phase_end():
            dnext[0] = 0
            sc.emit()

        def rstd_from_sq(st_ring_r, sq_list, Tsq, n_red, scale, recip=True):
            ps, Tps = next_psum()
            n = len(sq_list)
            for i, ap in enumerate(sq_list):
                sc.op("pe", OP("matmul", ps[:, :], lhsT=ones[:, :], rhs=ap, start=(i == 0), stop=(i == n - 1)),
                      reads=[Tconst, Tsq], writes=[Tps], signal=(i == n - 1))
            r, Tr = st_ring_r.next()
            sc.op("act", OP("activation", out=r[:, :], in_=ps[:, :], func=AF.Sqrt, scale=scale, bias=EPS), writes=[Tps, Tr])
            if recip:
                sc.op("dve", OP("reciprocal", out=r[:, :], in_=r[:, :]), writes=[Tr])
            return r, Tr

        def post_phase(l, y_name, g_post, x_in, x_out, g_pre, h_out, l_pre=None, S_tot=S):
            l_pre = l if l_pre is None else l_pre
            with ExitStack() as st:
                xr = Ring(st, 2, [128, 16, 512], F32, "x", dma=True)
                yr = Ring(st, 2, [128, 16, 512], F32, "y", dma=True) if y_name else None
                sqr = Ring(st, 1, [128, 16, 512], BF16, "sq")
                hr = Ring(st, 1, [128, 16, 512], BF16, "h", dma=True)
                rr = Ring(st, 3, [128, 512], F32, "rs")
                live = {}

                def stage_a(tb):
                    tsl = slice(tb * 512, (tb + 1) * 512)
                    x, Tx, Dx = xr.next()
                    sc.dma("sp", Dx, x[:, :, :], dr[x_in][:, tsl].rearrange("(k p) t -> p k t", p=128),
                           reads=dts(x_in, range(16), [tb]), writes=[Tx])
                    live[tb] = (x, Tx)
                    if not y_name:
                        return
                    y, Ty, Dy = yr.next()
                    sc.dma("sp", Dy, y[:, :, :], dr[y_name][:, tsl].rearrange("(k p) t -> p k t", p=128),
                           reads=dts(y_name, range(16), [tb]), writes=[Ty])
                    sq, Tsq = sqr.next()
                    for q4 in range(4):
                        sc.op("act", OP("activation", out=sq[:, 4 * q4:4 * q4 + 4, :], in_=y[:, 4 * q4:4 * q4 + 4, :], func=AF.Square),
                              reads=[Ty], writes=[Tsq])
                    r, Tr = rstd_from_sq(rr, [sq[:, k, :] for k in range(16)], Tsq, 16, 1.0 / D)
                    for k in range(16):
                        sc.op("dve", OP("scalar_tensor_tensor", out=y[:, k, :], in0=y[:, k, :], scalar=vcol(g_post, l, k), in1=r[:, :], op0=ALU.mult, op1=ALU.mult),
                              reads=[Tr, Tconst], writes=[Ty])
                    for (eng, k0, k1) in (("pool", 0, 5), ("dve", 10, 13), ("pool", 5, 10), ("dve", 13, 16)):
                        sc.op(eng, OP("tensor_tensor", out=x[:, k0:k1, :], in0=x[:, k0:k1, :], in1=y[:, k0:k1, :], op=ALU.add),
                              reads=[Ty], writes=[Tx])
                    sc.dma("sp", Dx, dr[x_out][:, tsl].rearrange("(k p) t -> p k t", p=128), x[:, :, :],
                           reads=[Tx], writes=dts(x_out, range(16), [tb]))

                def stage_b(tb):
                    tsl = slice(tb * 512, (tb + 1) * 512)
                    x, Tx = live.pop(tb)
                    if not g_pre:
                        return
                    sq, Tsq = sqr.next()
                    for q4 in range(4):
                        sc.op("act", OP("activation", out=sq[:, 4 * q4:4 * q4 + 4, :], in_=x[:, 4 * q4:4 * q4 + 4, :], func=AF.Square),
                              reads=[Tx], writes=[Tsq])
                    r, Tr = rstd_from_sq(rr, [sq[:, k, :] for k in range(16)], Tsq, 16, 1.0 / D)
                    h, Th, Dh = hr.next()
                    for k in range(16):
                        sc.op("dve", OP("scalar_tensor_tensor", out=h[:, k, :], in0=x[:, k, :], scalar=vcol(g_pre, l_pre, k), in1=r[:, :], op0=ALU.mult, op1=ALU.mult),
                              reads=[Tr, Tx, Tconst], writes=[Th])
                    sc.dma("sp", Dh, dr[h_out][:, tsl].rearrange("(k p) t -> p k t", p=128), h[:, :, :],
                           reads=[Th], writes=dts(h_out, range(16), [tb]))

                nt = S_tot // 512
                for tb in range(nt):
                    stage_a(tb)
                    if tb >= 1:
                        stage_b(tb - 1)
                stage_b(nt - 1)
                phase_end()

        def linear_phase(in_name, KC, w_ap, jlist, Tn, epi, S_tot=S, BLK=512, setup=None, half_setup=None, in_row0=0, w_reads=None):
            with ExitStack() as st:
                nblk = Tn // BLK
                A = alloc(st, [128, KC, Tn], BF16, "A")
                TA = [T() for _ in range(nblk)]
                DA = [dsem() for _ in range(nblk)]
                Wr = Ring(st, 4, [128, KC, 128], BF16, "W", dma=True)
                env = setup(st) if setup else None
                for th in range(S_tot // Tn):
                    for b in range(nblk):
                        t0 = th * Tn + b * BLK
                        tbs = sorted(set([t0 // 512, (t0 + BLK - 1) // 512]))
                        for k0 in range(0, KC, 16):
                            k1 = min(KC, k0 + 16)
                            sc.dma("sp", DA[b], A[:, k0:k1, b * BLK:(b + 1) * BLK],
                                   dr[in_name][(in_row0 + k0) * 128:(in_row0 + k1) * 128, t0:t0 + BLK].rearrange("(k p) t -> p k t", p=128),
                                   reads=dts(in_name, range(in_row0 + k0, in_row0 + k1), tbs), writes=[TA[b]])
                    if half_setup:
                        half_setup(env, th)
                    for j in jlist:
                        W, TW, DW = Wr.next()
                        for k0 in range(0, KC, 16):
                            k1 = min(KC, k0 + 16)
                            sc.dma("pool", DW, W[:, k0:k1, :], w_ap[k0 * 128:k1 * 128, j * 128:(j + 1) * 128].rearrange("(k p) n -> p k n", p=128),
                                   reads=(w_reads(j) if w_reads else ()), writes=[TW])
                        for b in range(nblk):
                            ps, Tps = next_psum(0, 4)
                            for k in range(KC):
                                sc.op("pe", OP("matmul", ps[:, 0:BLK], lhsT=W[:, k, :], rhs=A[:, k, b * BLK:(b + 1) * BLK], start=(k == 0), stop=(k == KC - 1)),
                                      reads=[TW, TA[b]], writes=[Tps], signal=(k == KC - 1))
                            epi(env, j, th * Tn + b * BLK, ps, Tps)
                phase_end()

        def mk_epi_store(dst, row_of_j, dt, nbuf=3):
            def setup(st):
                return Ring(st, nbuf, [128, 512], dt, "eo", dma=True)

            def epi(ring, j, tok0, ps, Tps, BLK=512):
                o, To, Do = ring.next()
                evac_copy(o[:, :], ps[:, :], Tps, To)
                r = row_of_j(j)
                sc.dma("sp", Do, dr[dst][r * 128:(r + 1) * 128, tok0:tok0 + 512], o[:, :], reads=[To], writes=dts(dst, [r], [tok0 // 512]))
            return setup, epi

        def mk_epi_sigmoid(l):
            def setup(st):
                return Ring(st, 3, [128, 512], BF16, "eo", dma=True)

            def epi(ring, j, tok0, ps, Tps):
                o, To, Do = ring.next()
                sc.op("act", OP("activation", out=o[:, :], in_=ps[:, :], func=AF.Sigmoid, bias=vcol("b_gate", l, j)), reads=[Tconst], writes=[Tps, To])
                sc.dma("sp", Do, dr["gT"][j * 128:(j + 1) * 128, tok0:tok0 + 512], o[:, :], reads=[To], writes=dts("gT", [j], [tok0 // 512]))
            return setup, epi

        def mk_epi_gate(l, goff, m_in, m_out):
            def setup(st):
                return (Ring(st, 3, [128, 512], BF16, "go", dma=True), Ring(st, 3, [128, 512], BF16, "gg", dma=True),
                        Ring(st, 3, [128, 512], BF16, "gm", dma=True), Ring(st, 2, [128, 512], F32, "gt"))

            def epi(env, j, tok0, ps, Tps):
                ro, rg, rm, rt = env
                tb = tok0 // 512
                g, Tg, Dg = rg.next()
                sc.dma("sp", Dg, g[:, :], dr["gT"][(goff + j) * 128:(goff + j + 1) * 128, tok0:tok0 + 512], reads=dts("gT", [goff + j], [tb]), writes=[Tg])
                o, To, Do = ro.next()
                if m_in is None:
                    sc.op("dve", OP("tensor_tensor", out=o[:, :], in0=ps[:, :], in1=g[:, :], op=ALU.mult), reads=[Tg], writes=[Tps, To])
                else:
                    m, Tm, Dm = rm.next()
                    sc.dma("sp", Dm, m[:, :], dr[m_in][j * 128:(j + 1) * 128, tok0:tok0 + 512], reads=dts(m_in, [j], [tb]), writes=[Tm])
                    t, Tt = rt.next()
                    sc.op("dve", OP("tensor_tensor", out=t[:, :], in0=ps[:, :], in1=g[:, :], op=ALU.mult), reads=[Tg], writes=[Tps, Tt])
                    sc.op("dve", OP("tensor_tensor", out=o[:, :], in0=t[:, :], in1=m[:, :], op=ALU.add), reads=[Tt, Tm], writes=[To])
                sc.dma("act", Do, dr[m_out][j * 128:(j + 1) * 128, tok0:tok0 + 512], o[:, :], reads=[To], writes=dts(m_out, [j], [tb]))
            return setup, epi

        def mk_epi_qk(l, Tn):
            def setup(st):
                env = Ctx()
                env.C = alloc(st, [128, Tn], F32, "ropeC")
                env.Sg = alloc(st, [128, Tn], F32, "ropeS")
                env.Tcs = T()
                env.Dcs = dsem()
                env.qg = Ring(st, 2, [128, 512], BF16, "qg")
                env.sq = Ring(st, 2, [128, 512], BF16, "sq")
                env.rr = Ring(st, 2, [128, 512], F32, "rr")
                env.t1 = Ring(st, 2, [128, 512], F32, "t1")
                env.t2 = Ring(st, 2, [128, 512], F32, "t2")
                env.o = Ring(st, 3, [128, 512], BF16, "qo", dma=True)
                env.th = 0
                return env

            def half_setup(env, th):
                env.th = th
                sc.dma("sp", env.Dcs, env.C[:, :], dr["ropeC"][:, th * Tn:(th + 1) * Tn], writes=[env.Tcs])
                sc.dma("sp", env.Dcs, env.Sg[:, :], dr["ropeS"][:, th * Tn:(th + 1) * Tn], writes=[env.Tcs])

            def epi(env, j, tok0, ps, Tps):
                gname = "q_norm_g" if j < 16 else "k_norm_g"
                dst, r = ("qT", j) if j < 16 else ("kT", j - 16)
                c0 = tok0 - env.th * Tn
                qg, Tqg = env.qg.next()
                sq, Tsq = env.sq.next()
                sc.op("act", OP("activation", out=qg[:, :], in_=ps[:, :], func=AF.Copy, scale=vcol(gname, l)), reads=[Tconst], writes=[Tps, Tqg])
                sc.op("act", OP("activation", out=sq[:, :], in_=ps[:, :], func=AF.Square), writes=[Tps, Tsq])
                rs, Trs = rstd_from_sq(env.rr, [sq[:, :]], Tsq, 1, 1.0 / 128)
                ps3, Tps3 = next_psum(4, 8)
                sc.op("pe", OP("matmul", ps3[:, :], lhsT=perm[:, :], rhs=qg[:, :], start=True, stop=True), reads=[Tconst, Tqg], writes=[Tps3])
                t1, Tt1 = env.t1.next()
                t2, Tt2 = env.t2.next()
                sc.op("dve", OP("tensor_tensor", out=t1[:, :], in0=qg[:, :], in1=env.C[:, c0:c0 + 512], op=ALU.mult), reads=[Tqg, env.Tcs], writes=[Tt1])
                sc.op("dve", OP("tensor_tensor", out=t2[:, :], in0=ps3[:, :], in1=env.Sg[:, c0:c0 + 512], op=ALU.mult), reads=[env.Tcs], writes=[Tps3, Tt2])
                sc.op("dve", OP("tensor_tensor", out=t1[:, :], in0=t1[:, :], in1=t2[:, :], op=ALU.add), reads=[Tt2], writes=[Tt1])
                o, To, Do = env.o.next()
                sc.op("dve", OP("tensor_tensor", out=o[:, :], in0=t1[:, :], in1=rs[:, :], op=ALU.mult), reads=[Tt1, Trs], writes=[To])
                sc.dma("sp", Do, dr[dst][r * 128:(r + 1) * 128, tok0:tok0 + 512], o[:, :], reads=[To], writes=dts(dst, [r], [tok0 // 512]))
            return setup, half_setup, epi

        def tokmajor_phase(in_name, KC, jobs, ntok, rhs_loader, dst):
            with ExitStack() as st:
                Ar = Ring(st, 1 if len(jobs) == 1 else 2, [128, KC, ntok], BF16, "A", dma=True)
                tbs = range((ntok + 511) // 512)
                R = alloc(st, [128, KC, 512], BF16, "R")
                TR = T()
                rhs_loader(R, TR)
                ring = Ring(st, 3, [128, 512], BF16, "to", dma=True)
                for in_row0, dst_col0, split in jobs:
                    A, TA, DA = Ar.next()
                    sc.dma("sp", DA, A[:, :, :], dr[in_name][in_row0 * 128:(in_row0 + KC) * 128, 0:ntok].rearrange("(k p) t -> p k t", p=128),
                           reads=dts(in_name, range(in_row0, in_row0 + KC), tbs), writes=[TA])
                    for tt in range(ntok // 128):
                        ps, Tps = next_psum()
                        for k in range(KC):
                            sc.op("pe", OP("matmul", ps[:, :], lhsT=A[:, k, tt * 128:(tt + 1) * 128], rhs=R[:, k, :], start=(k == 0), stop=(k == KC - 1)),
                                  reads=[TA, TR], writes=[Tps], signal=(k == KC - 1))
                        o, To, Do = ring.next()
                        evac_copy(o[:, :], ps[:, :], Tps, To)
                        if split is None:
                            sc.dma("sp", Do, dr[dst][tt * 128:(tt + 1) * 128, dst_col0:dst_col0 + 512], o[:, :], reads=[To],
                                   writes=dts(dst, [("tm", dst_col0)], [tt // 4]))
                        else:
                            sc.dma("sp", Do, dr[dst][tt * 128:(tt + 1) * 128, split:split + 256], o[:, 0:256], reads=[To],
                                   writes=dts(dst, [("ab", split)], [tt // 4]))
                            sc.dma("sp", Do, dr[dst][S + tt * 128:S + (tt + 1) * 128, split:split + 256], o[:, 256:512], reads=[To],
                                   writes=dts(dst, [("ab", split)], [tt // 4]))
                phase_end()

        def attention_phase(q_name, k_name, v_name, o_name, n_kvh, group, nkc):
            nkey = nkc * 128
            with ExitStack() as st:
                Kr = Ring(st, 2, [128, nkey], BF16, "K", dma=True)
                Vr = Ring(st, 2, [128, nkc, 128], BF16, "V", dma=True)
                Qr = Ring(st, 3, [128, 512], BF16, "Q", dma=True)
                Pr = Ring(st, 8, [128, 512], BF16, "P")
                Of = Ring(st, 2, [128, 512], F32, "Of")
                Or = Ring(st, 2, [128, 512], BF16, "O", dma=True)
                Rr = Ring(st, 2, [128, 512], F32, "R")
                ktb = range((nkey + 511) // 512)
                oi = 0
                for kh in range(n_kvh):
                    Kt, TK, DK = Kr.next()
                    sc.dma("sp", DK, Kt[:, :], dr[k_name][kh * 128:(kh + 1) * 128, 0:nkey], reads=dts(k_name, [kh], ktb), writes=[TK])
                    Vt, TV, DV = Vr.next()
                    sc.dma("sp", DV, Vt[:, :, :], dr[v_name][0:nkey, kh * 128:(kh + 1) * 128].rearrange("(c p) d -> p c d", p=128),
                           reads=dts(v_name, [("tm", 0)], ktb), writes=[TV])
                    for hq in range(group):
                        h = kh * group + hq
                        for qb in range(S // 512):
                            Q, TQ, DQ = Qr.next()
                            sc.dma("sp", DQ, Q[:, :], dr[q_name][h * 128:(h + 1) * 128, qb * 512:(qb + 1) * 512], reads=dts(q_name, [h], [qb]), writes=[TQ])
                            pso, Tpso = psum[6], Tpsum[6]
                            pss, Tpss = psum[7], Tpsum[7]
                            oi += 1
                            pend = []

                            def qk(kc):
                                ps, Tps = next_psum(0, 6)
                                sc.op("pe", OP("matmul", ps[:, :], lhsT=Kt[:, kc * 128:(kc + 1) * 128], rhs=Q[:, :], start=True, stop=True),
                                      reads=[TK, TQ], writes=[Tps])
                                P, TP = Pr.next()
                                sc.op("act", OP("activation", out=P[:, :], in_=ps[:, :], func=AF.Exp, scale=float(128 ** -0.5)), writes=[Tps, TP])
                                pend.append((kc, P, TP))

                            def pv(merge=False):
                                kc, P, TP = pend.pop(0)
                                extra = [pend[0][2]] if (merge and pend) else []
                                sc.op("pe", OP("matmul", pso[:, :], lhsT=Vt[:, kc, :], rhs=P[:, :], start=(kc == 0), stop=(kc == nkc - 1)),
                                      reads=[TV, TP] + extra, writes=[Tpso], signal=(kc == nkc - 1))
                                sc.op("pe", OP("matmul", pss[:, :], lhsT=ones[:, :], rhs=P[:, :], start=(kc == 0), stop=(kc == nkc - 1)),
                                      reads=[Tconst, TP], writes=[Tpss], signal=(kc == nkc - 1))

                            for kc in range(0, nkc, 2):
                                qk(kc)
                                qk(kc + 1)
                                if len(pend) > 4:
                                    pv(True)
                                    pv()
                            while pend:
                                pv(True)
                                pv()
                            R, TR = Rr.next()
                            sc.op("dve", OP("tensor_copy", out=R[:, :], in_=pss[:, :]), writes=[Tpss, TR])
                            of, Tof = Of.next()
                            sc.op("dve", OP("tensor_copy", out=of[:, :], in_=pso[:, :]), writes=[Tpso, Tof])
                            sc.op("dve", OP("reciprocal", out=R[:, :], in_=R[:, :]), writes=[TR])
                            O, TO, DO = Or.next()
                            sc.op("dve", OP("tensor_tensor", out=O[:, :], in0=of[:, :], in1=R[:, :], op=ALU.mult), reads=[TR, Tof], writes=[TO])
                            sc.dma("sp", DO, dr[o_name][h * 128:(h + 1) * 128, qb * 512:(qb + 1) * 512], O[:, :], reads=[TO], writes=dts(o_name, [h], [qb]))
                phase_end()

        def fourier_phase():
            with ExitStack() as st:
                CS = alloc(st, [128, 2, 512], BF16, "CS")
                TCS, DCS = T(), dsem()
                sc.dma("sp", DCS, CS[:, :, :], dr["cs256"].rearrange("(k p) n -> p k n", p=128), writes=[TCS])
                Ur = Ring(st, 2, [128, 2, S], BF16, "U", dma=True)
                ABr = Ring(st, 2, [128, 32, 512], BF16, "AB")
                Tr_ = Ring(st, 3, [128, 16, 512], BF16, "Tb", dma=True)
                Yo = Ring(st, 3, [128, 512], BF16, "Yo", dma=True)
                for g in range(4):
                    U, TU, DU = Ur.next()
                    sc.dma("sp", DU, U[:, :, :], dr["ufT"][g * 256:(g + 1) * 256, :].rearrange("(k p) t -> p k t", p=128),
                           reads=dts("ufT", [2 * g, 2 * g + 1], range(8)), writes=[TU])
                    AB, TAB = ABr.next()
                    for tt in range(32):
                        ps, Tps = next_psum(0, 4)
                        for c2 in range(2):
                            sc.op("pe", OP("matmul", ps[:, :], lhsT=U[:, c2, tt * 128:(tt + 1) * 128], rhs=CS[:, c2, :], start=(c2 == 0), stop=(c2 == 1)),
                                  reads=[TU, TCS], writes=[Tps], signal=(c2 == 1))
                        evac_copy(AB[:, tt, :], ps[:, :], Tps, TAB)
                    for sb in range(8):
                        acc = [(psum[4 + 2 * (sb % 2) + c], Tpsum[4 + 2 * (sb % 2) + c]) for c in range(2)]
                        step = 0
                        for tab in ("dftC", "dftS"):
                            for hs in range(2):
                                Tb, TTb, DTb = Tr_.next()
                                sc.dma("sp", DTb, Tb[:, :, :], dr[tab][hs * 2048:(hs + 1) * 2048, sb * 512:(sb + 1) * 512].rearrange("(k p) n -> p k n", p=128), writes=[TTb])
                                for k in range(16):
                                    scn = hs * 16 + k
                                    for c in range(2):
                                        off = (0 if tab == "dftC" else 256) + c * 128
                                        first = (step == 0)
                                        last = (tab == "dftS" and hs == 1 and k == 15)
                                        sc.op("pe", OP("matmul", acc[c][0][:, :], lhsT=AB[:, scn, off:off + 128], rhs=Tb[:, k, :], start=first, stop=last),
                                              reads=[TAB, TTb], writes=[acc[c][1]], signal=last)
                                    step += 1
                        for c in range(2):
                            o, To, Do = Yo.next()
                            evac_copy(o[:, :], acc[c][0][:, :], acc[c][1], To)
                            r = 2 * g + c
                            sc.dma("sp", Do, dr["YT"][r * 128:(r + 1) * 128, sb * 512:(sb + 1) * 512], o[:, :], reads=[To], writes=dts("YT", [r], [sb]))
                phase_end()

        def ffn_up_phase(l):
            Tn = 2048
            NC_ = Tn + 2
            w_up = dr["w_up"][l]
            with ExitStack() as st:
                A = alloc(st, [128, 16, NC_], BF16, "A")
                TA, DA = T(), dsem()
                Wr = Ring(st, 4, [128, 16, 128], BF16, "W", dma=True)
                Ur = Ring(st, 3, [128, NC_], F32, "U")
                Cr = Ring(st, 3, [128, Tn], F32, "Cv")
                Gr = Ring(st, 2, [128, Tn], F32, "Gl")
                Or = Ring(st, 2, [128, Tn], BF16, "Ao", dma=True)
                for th in range(2):
                    t_lo = th * Tn - 1
                    if th == 0:
                        sc.op("pool", OP("memset", A[:, :, 0:1], 0.0), writes=[TA])
                        sc.dma("sp", DA, A[:, :, 1:NC_], dr["hT"][:, 0:Tn + 1].rearrange("(k p) t -> p k t", p=128),
                               reads=dts("hT", range(16), range(0, 5)), writes=[TA])
                    else:
                        sc.op("pool", OP("memset", A[:, :, NC_ - 1:NC_], 0.0), writes=[TA])
                        sc.dma("sp", DA, A[:, :, 0:NC_ - 1], dr["hT"][:, t_lo:S].rearrange("(k p) t -> p k t", p=128),
                               reads=dts("hT", range(16), range(3, 8)), writes=[TA])
                    for i in range(NFF):
                        res = []
                        for part in range(2):
                            jc = part * NFF + i
                            W, TW, DW = Wr.next()
                            sc.dma("pool", DW, W[:, :, :], w_up[:, jc * 128:(jc + 1) * 128].rearrange("(k p) n -> p k n", p=128), writes=[TW])
                            U, TU = Ur.next()
                            cv, Tcv = Cr.next()
                            for b in range(5):
                                c0 = b * 512
                                n = 512 if b < 4 else 2
                                ps, Tps = next_psum()
                                for k in range(16):
                                    sc.op("pe", OP("matmul", ps[:, 0:n], lhsT=W[:, k, :], rhs=A[:, k, c0:c0 + n], start=(k == 0), stop=(k == 15)),
                                          reads=[TW, TA], writes=[Tps], signal=(k == 15))
                                sc.op("act", OP("activation", out=U[:, c0:c0 + n], in_=ps[:, 0:n], func=AF.Copy), reads=[Tconst], writes=[Tps, TU])
                            sc.op("act", OP("activation", out=cv[:, :], in_=U[:, 1:Tn + 1], func=AF.Identity, scale=vcol("conv_w1", l, jc), bias=vcol("conv_b", l, jc)),
                                  reads=[TU, Tconst], writes=[Tcv])
                            sc.op("dve", OP("scalar_tensor_tensor", out=cv[:, :], in0=U[:, 0:Tn], scalar=vcol("conv_w0", l, jc), in1=cv[:, :], op0=ALU.mult, op1=ALU.add),
                                  reads=[TU, Tconst], writes=[Tcv])
                            sc.op("dve", OP("scalar_tensor_tensor", out=cv[:, :], in0=U[:, 2:Tn + 2], scalar=vcol("conv_w2", l, jc), in1=cv[:, :], op0=ALU.mult, op1=ALU.add),
                                  reads=[TU, Tconst], writes=[Tcv])
                            res.append((cv, Tcv))
                        gl, Tgl = Gr.next()
                        (cg, Tcg), (cvv, Tcvv) = res
                        sc.op("act", OP("activation", out=gl[:, :], in_=cg[:, :], func=AF.Gelu_apprx_tanh), reads=[Tcg], writes=[Tgl])
                        o, To, Do = Or.next()
                        sc.op("dve", OP("tensor_tensor", out=o[:, :], in0=gl[:, :], in1=cvv[:, :], op=ALU.mult), reads=[Tgl, Tcvv], writes=[To])
                        sc.dma("sp", Do, dr["actT"][i * 128:(i + 1) * 128, th * Tn:(th + 1) * Tn], o[:, :], reads=[To],
                               writes=dts("actT", [i], range(th * 4, th * 4 + 4)))
                phase_end()

        def mem_phase(l):
            with ExitStack() as st:
                x = alloc(st, [128, 16, 256], F32, "mx")
                sq = alloc(st, [128, 16, 256], BF16, "msq")
                h = alloc(st, [128, 16, 256], BF16, "mh")
                r = alloc(st, [128, 256], F32, "mr")
                Tx, Tsq, Th, Tr, Dx = T(), T(), T(), T(), dsem()
                sc.dma("sp", Dx, x[:, :, :], dr["memT"].rearrange("(k p) t -> p k t", p=128), writes=[Tx])
                sc.op("act", OP("activation", out=sq[:, :, :], in_=x[:, :, :], func=AF.Square), reads=[Tx], writes=[Tsq])
                ps, Tps = next_psum()
                for k in range(16):
                    sc.op("pe", OP("matmul", ps[:, 0:256], lhsT=ones[:, :], rhs=sq[:, k, :], start=(k == 0), stop=(k == 15)),
                          reads=[Tconst, Tsq], writes=[Tps], signal=(k == 15))
                sc.op("act", OP("activation", out=r[:, :], in_=ps[:, 0:256], func=AF.Sqrt, scale=1.0 / D, bias=EPS), writes=[Tps, Tr])
                sc.op("dve", OP("reciprocal", out=r[:, :], in_=r[:, :]), writes=[Tr])
                for k in range(16):
                    sc.op("dve", OP("scalar_tensor_tensor", out=h[:, k, :], in0=x[:, k, :], scalar=vcol("mem_norm_g", l, k), in1=r[:, :], op0=ALU.mult, op1=ALU.mult),
                          reads=[Tr, Tx, Tconst], writes=[Th])
                sc.dma("sp", Dx, dr["memnT"].rearrange("(k p) t -> p k t", p=128), h[:, :, :], reads=[Th], writes=dts("memnT", range(16), [0]))
                phase_end()

        phases = []

        def run(name, fn, *a, **kw):
            phases.append(name)
            if stop_after is not None and len(phases) > stop_after:
                return
            if only is not None and name not in only:
                return
            fn(*a, **kw)

        x_cur = "xT"
        run("pre0", post_phase, 0, None, None, x_cur, None, "mix_pre_g", "hT")
        for l in range(n_layers):
            su, hs, ep = mk_epi_qk(l, 2048)
            run("qk", linear_phase, "hT", 16, dr["w_in"][l], list(range(20)), 2048, ep, setup=su, half_setup=hs)
            su, ep = mk_epi_store("ufT", lambda j: j - 24, BF16)
            run("uf", linear_phase, "hT", 16, dr["w_in"][l], list(range(24, 32)), 2048, ep, setup=su)
            su, ep = mk_epi_sigmoid(l)
            run("gate", linear_phase, "hT", 16, dr["w_gate"][l], list(range(32)), 2048, ep, setup=su)
            def vload(R, TR, l=l):
                sc.dma("pool", dsem(), R[:, :, :], dr["w_in"][l][:, 2560:3072].rearrange("(k p) n -> p k n", p=128), writes=[TR])
            run("v", tokmajor_phase, "hT", 16, [(0, 0, None)], S, vload, "vtm")
            run("attn", attention_phase, "qT", "kT", "vtm", "oT", 4, 4, 32)
            def csload(R, TR):
                sc.dma("sp", dsem(), R[:, :, :], dr["cs256"].rearrange("(k p) n -> p k n", p=128), writes=[TR])
            run("fab", tokmajor_phase, "ufT", 2, [(2 * g, 0, g * 256) for g in range(4)], S, csload, "abS")
            su, ep = mk_epi_store("YT", lambda j: j, BF16)
            run("fdft", linear_phase, "dftCS", 64, dr["abS"], list(range(8)), 512, ep, setup=su,
                w_reads=lambda j: dts("abS", [("ab", (j // 2) * 256)], range(8)))
            su, ep = mk_epi_gate(l, 0, None, "m1T")
            run("ao", linear_phase, "oT", 16, dr["w_attn_o"][l], list(range(16)), 2048, ep, setup=su)
            su, ep = mk_epi_gate(l, 16, "m1T", "mT")
            run("fo", linear_phase, "YT", 8, dr["w_four_o"][l], list(range(16)), 2048, ep, setup=su)
            su, ep = mk_epi_store("yT", lambda j: j, F32)
            run("mix", linear_phase, "mT", 16, dr["w_mix_o"][l], list(range(16)), 2048, ep, setup=su)
            x_nxt = "xa"
            run("post1", post_phase, l, "yT", "mix_post_g", x_cur, x_nxt, "xa_pre_g", "hT")
            x_cur = x_nxt
            run("mem", mem_phase, l)
            su, ep = mk_epi_store("qxT", lambda j: j, BF16)
            run("xq", linear_phase, "hT", 16, dr["w_xq"][l], list(range(4)), 2048, ep, setup=su)

            def mk_small_store(dst):
                def setup(st):
                    return Ring(st, 3, [128, 256], BF16, "ko", dma=True)

                def epi(ring, j, tok0, ps, Tps):
                    o, To, Do = ring.next()
                    evac_copy(o[:, :], ps[:, 0:256], Tps, To)
                    sc.dma("sp", Do, dr[dst][j * 128:(j + 1) * 128, 0:256], o[:, :], reads=[To], writes=dts(dst, [j], [0]))
                return setup, epi
            su, ep = mk_small_store("kxT")
            run("xk", linear_phase, "memnT", 16, dr["w_xkv"][l], list(range(4)), 256, ep, S_tot=256, BLK=256, setup=su)

            def vxload(R, TR, l=l):
                sc.dma("pool", dsem(), R[:, :, :], dr["w_xkv"][l][:, 512:1024].rearrange("(k p) n -> p k n", p=128), writes=[TR])
            run("xv", tokmajor_phase, "memnT", 16, [(0, 0, None)], 256, vxload, "vxtm")
            run("xattn", attention_phase, "qxT", "kxT", "vxtm", "oxT", 4, 1, 2)
            su, ep = mk_epi_store("yT", lambda j: j, F32)
            run("xo", linear_phase, "oxT", 4, dr["w_xo"][l], list(range(16)), 4096, ep, setup=su)
            x_nxt = "xb"
            run("post2", post_phase, l, "yT", "xa_post_g", x_cur, x_nxt, "ffn_pre_g", "hT")
            x_cur = x_nxt
            run("ffn_up", ffn_up_phase, l)
            su, ep = mk_epi_store("yT", lambda j: j, F32)
            run("ffn_down", linear_phase, "actT", NFF, dr["w_down"][l], list(range(16)), 1024, ep, setup=su)
            last = (l == n_layers - 1)
            x_nxt = "outT" if last else "xa"
            run("post3", post_phase, l, "yT", "ffn_post_g", x_cur, x_nxt, None if last else "mix_pre_g", "hT", l_pre=l + 1)
            x_cur = x_nxt
    ctx.phases = phases
    return nc, ctx


WNAMES = ["w_in", "w_attn_o", "w_four_o", "w_gate", "w_mix_o", "w_xq", "w_xkv", "w_xo", "w_up", "w_down"]


def make_in_map(inp, b, tabs, vecs):
    m = {"xT": np.ascontiguousarray(np.asarray(inp["x"][b], np.float32).T),
         "memT": np.ascontiguousarray(np.asarray(inp["mem"][b], np.float32).T),
         "vecs": vecs}
    m.update(tabs)
    for w in WNAMES:
        m[w] = np.ascontiguousarray(np.asarray(inp[w], np.float32))
    return m


def kernel(**inputs):
    inp = {k: np.asarray(v) for k, v in inputs.items()}
    tabs = host_tables()
    vecs = pack_vecs(inp)
    nc, _ = build_program(n_layers=L)
    base = make_in_map(inp, 0, tabs, vecs)
    in_maps = []
    for b in range(NCORES):
        m = dict(base)
        m["xT"] = np.ascontiguousarray(np.asarray(inp["x"][b], np.float32).T)
        m["memT"] = np.ascontiguousarray(np.asarray(inp["mem"][b], np.float32).T)
        in_maps.append(m)
    res = run_bass_kernel_spmd(nc, in_maps, core_ids=list(range(NCORES)))
    out = np.empty((NCORES, S, D), np.float32)
    for b in range(NCORES):
        out[b] = np.asarray(res.results[b]["outT"], np.float32).T
    return out
```
